# Optimizing a Trainium2 kernel written in Bass

```python
import jax, jax.numpy as jnp
from jax import lax
import numpy as np

D_MODEL = 1024
BATCH = 8
SEQ = 2048
DEPTH = 2
DEC_BATCH = 128
DEC_SEQ = 1
PAST_LEN = 16384
PAGE_SIZE = 128

N_MIXERS = 2
N_MLSTM_LAYERS = (DEPTH + 1) // 2
N_CONV_LAYERS = DEPTH // 2
INNER = 2 * D_MODEL
MLSTM_HEADS = 4
MLSTM_DH = INNER // MLSTM_HEADS
MLSTM_CONV = 4
CHUNK = 64
CONV_WIDTH = 31
EPS = 1e-6
NEG = -1e30

kernel_name = "hybrid_mlstm_conformer_adaln_step"


def rmsnorm(x, g):
    xf = x.astype(jnp.float32)
    y = xf * lax.rsqrt(jnp.mean(xf * xf, axis=-1, keepdims=True) + EPS)
    return (y * g).astype(x.dtype)


def layernorm(x, g, b=None):
    xf = x.astype(jnp.float32)
    mu = jnp.mean(xf, axis=-1, keepdims=True)
    var = jnp.mean(jnp.square(xf - mu), axis=-1, keepdims=True)
    y = (xf - mu) * lax.rsqrt(var + EPS) * g
    if b is not None:
        y = y + b
    return y.astype(x.dtype)


def adaln(c, w, b):
    mod = jax.nn.silu(c) @ w + b
    shift, scale, gate = jnp.split(mod, 3, axis=-1)
    return shift[:, None, :], scale[:, None, :], gate[:, None, :]


def causal_dwconv(x_full, w):
    return lax.conv_general_dilated(
        x_full, w[:, None, :].astype(x_full.dtype), (1,), 'VALID',
        dimension_numbers=('NWC', 'WIO', 'NWC'), feature_group_count=x_full.shape[-1])


def mlstm_chunk(C, n, m, q, k, v, ig, lf):
    L = q.shape[2]
    b = jnp.cumsum(lf, axis=-1)
    causal = jnp.tril(jnp.ones((L, L), dtype=bool))
    dmat = jnp.where(causal, b[..., :, None] - b[..., None, :] + ig[..., None, :], NEG)
    inter = b + m[..., None]
    m_t = jnp.maximum(inter, jnp.max(dmat, axis=-1))
    w_intra = jnp.exp(dmat - m_t[..., None])
    w_inter = jnp.exp(inter - m_t)
    s = jnp.einsum('bhtd,bhsd->bhts', q, k) * w_intra
    num = jnp.einsum('bhts,bhsd->bhtd', s, v) + w_inter[..., None] * jnp.einsum('bhtk,bhkv->bhtv', q, C)
    den = jnp.sum(s, axis=-1) + w_inter * jnp.einsum('bhtk,bhk->bht', q, n)
    h = num / jnp.maximum(jnp.abs(den), jnp.exp(-m_t))[..., None]
    m_new = m_t[..., -1]
    b_last = b[..., -1]
    decay = jnp.exp(b_last + m - m_new)
    wk = jnp.exp(b_last[..., None] - b + ig - m_new[..., None])
    C_new = decay[..., None, None] * C + jnp.einsum('bhs,bhsk,bhsv->bhkv', wk, k, v)
    n_new = decay[..., None] * n + jnp.einsum('bhs,bhsk->bhk', wk, k)
    return h, C_new, n_new, m_new


def mlstm_cell(q, k, v, ig, lf, C, n, m):
    B, H, T, DH = q.shape
    L = CHUNK if T % CHUNK == 0 else T
    NC = T // L

    def to_chunks(a):
        return jnp.moveaxis(a.reshape((B, H, NC, L) + a.shape[3:]), 2, 0)

    def step(carry, xs):
        Cc, nc, mc = carry
        h, Cc, nc, mc = mlstm_chunk(Cc, nc, mc, *xs)
        return (Cc, nc, mc), h

    (C, n, m), hs = lax.scan(step, (C, n, m), tuple(to_chunks(a) for a in (q, k, v, ig, lf)))
    h = jnp.moveaxis(hs, 0, 2).reshape(B, H, T, DH)
    return h, C, n, m


def mlstm_branch(h, conv_buf, C, n, m, w_in, w_conv, b_conv, w_q, w_k, w_v,
                 w_ig, b_ig, w_fg, b_fg, ln_g, skip, w_out):
    B, T, _ = h.shape
    xm, z = jnp.split(h @ w_in, 2, axis=-1)
    xfull = jnp.concatenate([conv_buf.astype(xm.dtype), xm], axis=1)
    xc = jax.nn.silu(causal_dwconv(xfull, w_conv) + b_conv)
    xch = xc.reshape(B, T, MLSTM_HEADS, MLSTM_DH)
    q = jnp.einsum('bthd,hde->bthe', xch, w_q)
    k = jnp.einsum('bthd,hde->bthe', xch, w_k)
    v = jnp.einsum('bthd,hde->bthe', xm.reshape(B, T, MLSTM_HEADS, MLSTM_DH), w_v)
    qkv = jnp.concatenate([q.reshape(B, T, INNER), k.reshape(B, T, INNER), v.reshape(B, T, INNER)], axis=-1)
    ig = qkv @ w_ig + b_ig
    lf = jax.nn.log_sigmoid((qkv @ w_fg + b_fg).astype(jnp.float32))

    def tr(a):
        return jnp.moveaxis(a.astype(jnp.float32), 1, 2)

    hh, C, n, m = mlstm_cell(tr(q), tr(k) * (MLSTM_DH ** -0.5), tr(v), tr(ig), tr(lf),
                             C.astype(jnp.float32), n.astype(jnp.float32), m.astype(jnp.float32))
    hh = layernorm(hh, ln_g.astype(jnp.float32)[:, None, :])
    hh = jnp.moveaxis(hh, 1, 2).reshape(B, T, INNER).astype(xm.dtype)
    out = ((hh + skip * xc) * jax.nn.silu(z)) @ w_out
    return out, C, n, m, xfull[:, -(MLSTM_CONV - 1):]


def conformer_branch(h, conv_buf, w_in, b_in, w_dw, b_dw, ln_g, ln_b, w_out):
    a, g, z = jnp.split(h @ w_in + b_in, 3, axis=-1)
    u = a * jax.nn.sigmoid(g)
    ufull = jnp.concatenate([conv_buf.astype(u.dtype), u], axis=1)
    y = causal_dwconv(ufull, w_dw) + b_dw
    y = layernorm(y, ln_g, ln_b)
    out = (jax.nn.silu(y) * jax.nn.silu(z)) @ w_out
    return out, ufull[:, -(CONV_WIDTH - 1):]


def setup_inputs(seed: int = 0) -> dict:
    key = jax.random.key(seed)
    keys = jax.random.split(key, 48)
    cnt = [0]

    def nk():
        cnt[0] += 1
        return keys[cnt[0] - 1]

    def nrm(shape, scale):
        return jax.random.normal(nk(), shape, jnp.float32) * scale

    NA, NB, E, H, DH, D = N_MLSTM_LAYERS, N_CONV_LAYERS, INNER, MLSTM_HEADS, MLSTM_DH, D_MODEL
    inp = {}
    inp['x_prompt'] = nrm((BATCH, SEQ, D), 1.0)
    inp['x_sample'] = nrm((DEC_BATCH, DEC_SEQ, D), 1.0)
    inp['c_prompt'] = nrm((BATCH, D), 1.0)
    inp['c_sample'] = nrm((DEC_BATCH, D), 1.0)
    inp['state_mlstm_C'] = nrm((NA, DEC_BATCH, H, DH, DH), 0.05)
    inp['state_mlstm_n'] = nrm((NA, DEC_BATCH, H, DH), 0.1)
    inp['state_mlstm_m'] = jax.random.uniform(nk(), (NA, DEC_BATCH, H), jnp.float32, 0.0, 2.0)
    inp['state_mlstm_conv'] = nrm((NA, DEC_BATCH, MLSTM_CONV - 1, E), 1.0)
    inp['state_conf_conv'] = nrm((NB, DEC_BATCH, CONV_WIDTH - 1, E), 0.5)
    inp['norm_g'] = 1.0 + nrm((DEPTH, D), 0.02)
    inp['w_ada'] = nrm((DEPTH, D, 3 * D), 0.5 * D ** -0.5)
    inp['b_ada'] = nrm((DEPTH, 3 * D), 0.02)
    inp['ml_w_in'] = nrm((NA, D, 2 * E), D ** -0.5)
    inp['ml_w_conv'] = nrm((NA, MLSTM_CONV, E), MLSTM_CONV ** -0.5)
    inp['ml_b_conv'] = nrm((NA, E), 0.02)
    inp['ml_w_q'] = nrm((NA, H, DH, DH), DH ** -0.5)
    inp['ml_w_k'] = nrm((NA, H, DH, DH), DH ** -0.5)
    inp['ml_w_v'] = nrm((NA, H, DH, DH), DH ** -0.5)
    inp['ml_w_ig'] = nrm((NA, 3 * E, H), (3 * E) ** -0.5)
    inp['ml_b_ig'] = nrm((NA, H), 0.1)
    inp['ml_w_fg'] = nrm((NA, 3 * E, H), (3 * E) ** -0.5)
    inp['ml_b_fg'] = jnp.broadcast_to(jnp.linspace(3.0, 6.0, H, dtype=jnp.float32), (NA, H)) + nrm((NA, H), 0.1)
    inp['ml_ln_g'] = 1.0 + nrm((NA, H, DH), 0.02)
    inp['ml_skip'] = 1.0 + nrm((NA, E), 0.02)
    inp['ml_w_out'] = nrm((NA, E, D), E ** -0.5)
    inp['cf_w_in'] = nrm((NB, D, 3 * E), D ** -0.5)
    inp['cf_b_in'] = nrm((NB, 3 * E), 0.02)
    inp['cf_w_dw'] = nrm((NB, CONV_WIDTH, E), CONV_WIDTH ** -0.5)
    inp['cf_b_dw'] = nrm((NB, E), 0.02)
    inp['cf_ln_g'] = 1.0 + nrm((NB, E), 0.02)
    inp['cf_ln_b'] = nrm((NB, E), 0.02)
    inp['cf_w_out'] = nrm((NB, E, D), E ** -0.5)
    inp['final_g'] = 1.0 + nrm((D,), 0.02)
    return inp


def reference(x_prompt, x_sample, c_prompt, c_sample, state_mlstm_C, state_mlstm_n, state_mlstm_m,
              state_mlstm_conv, state_conf_conv, norm_g, w_ada, b_ada, ml_w_in, ml_w_conv, ml_b_conv,
              ml_w_q, ml_w_k, ml_w_v, ml_w_ig, ml_b_ig, ml_w_fg, ml_b_fg, ml_ln_g, ml_skip, ml_w_out,
              cf_w_in, cf_b_in, cf_w_dw, cf_b_dw, cf_ln_g, cf_ln_b, cf_w_out, final_g):

    def run(x, c, mC, mn, mm, mconv, cconv):
        out_C, out_n, out_m, out_mconv, out_cconv = [], [], [], [], []
        for i in range(DEPTH):
            shift, scale, gate = adaln(c, w_ada[i], b_ada[i])
            h = rmsnorm(x, norm_g[i]) * (1.0 + scale) + shift
            j = i // N_MIXERS
            if i % N_MIXERS == 0:
                out, C, n, m, buf = mlstm_branch(
                    h, mconv[j], mC[j], mn[j], mm[j], ml_w_in[j], ml_w_conv[j], ml_b_conv[j],
                    ml_w_q[j], ml_w_k[j], ml_w_v[j], ml_w_ig[j], ml_b_ig[j], ml_w_fg[j], ml_b_fg[j],
                    ml_ln_g[j], ml_skip[j], ml_w_out[j])
                out_C.append(C); out_n.append(n); out_m.append(m); out_mconv.append(buf)
            else:
                out, buf = conformer_branch(h, cconv[j], cf_w_in[j], cf_b_in[j], cf_w_dw[j],
                                            cf_b_dw[j], cf_ln_g[j], cf_ln_b[j], cf_w_out[j])
                out_cconv.append(buf)
            x = x + gate * out
        y = rmsnorm(x, final_g)
        return (y, jnp.stack(out_C), jnp.stack(out_n), jnp.stack(out_m),
                jnp.stack(out_mconv), jnp.stack(out_cconv))

    B = x_prompt.shape[0]
    f32 = jnp.float32
    p_C0 = jnp.zeros((N_MLSTM_LAYERS, B, MLSTM_HEADS, MLSTM_DH, MLSTM_DH), f32)
    p_n0 = jnp.zeros((N_MLSTM_LAYERS, B, MLSTM_HEADS, MLSTM_DH), f32)
    p_m0 = jnp.zeros((N_MLSTM_LAYERS, B, MLSTM_HEADS), f32)
    p_mc0 = jnp.zeros((N_MLSTM_LAYERS, B, MLSTM_CONV - 1, INNER), x_prompt.dtype)
    p_cc0 = jnp.zeros((N_CONV_LAYERS, B, CONV_WIDTH - 1, INNER), x_prompt.dtype)

    y_prompt, C_p, n_p, m_p, mconv_p, cconv_p = run(x_prompt, c_prompt, p_C0, p_n0, p_m0, p_mc0, p_cc0)
    y_sample, C_s, n_s, m_s, mconv_s, cconv_s = run(x_sample, c_sample, state_mlstm_C, state_mlstm_n,
                                                    state_mlstm_m, state_mlstm_conv, state_conf_conv)
    return (y_prompt, y_sample, C_p, C_s, n_p, n_s, m_p, m_s, mconv_p, mconv_s, cconv_p, cconv_s)
```

```python
import numpy as np
import concourse.bass as bass
import concourse.mybir as mybir
from concourse.bass_utils import run_bass_kernel_spmd

F32 = mybir.dt.float32
BF16 = mybir.dt.bfloat16
ALU = mybir.AluOpType
AF = mybir.ActivationFunctionType
AX = mybir.AxisListType

D = 1024
E = 2048
H = 4
DH = 512
SEQ = 2048
NS = 16
TB = 512
NBLK = SEQ // TB
EPS = 1e-6
BIG = 30000.0
NCORES = 8
NWC = 40
WCACHE = False


class Buf:
    __slots__ = ("name", "w", "r")

    def __init__(self, name):
        self.name = name
        self.w = None
        self.r = {}


class Eng:
    def __init__(self, trk, eng, name, is_pe=False):
        self.trk = trk
        self.eng = eng
        self.name = name
        self.key = "E" + name
        self.cnt = 0
        self.seen = {}
        self.is_pe = is_pe
        if not trk.dry:
            trk.sems[self.key] = trk.nc.alloc_semaphore("s_" + name)

    def wait(self, ev):
        if ev is None:
            return
        key, val = ev
        if self.is_pe and key == self.key:
            return
        if self.seen.get(key, 0) >= val:
            return
        self.seen[key] = val
        if not self.trk.dry:
            self.eng.wait_ge(self.trk.sems[key], val)
        self.trk.nwaits += 1


class Trk:
    def __init__(self, nc, dry, n_dma_sems=32):
        self.nc = nc
        self.dry = dry
        self.sems = {}
        self.nwaits = 0
        self.nops = 0
        self.pe = Eng(self, nc.tensor, "pe", is_pe=True)
        self.dve = Eng(self, nc.vector, "dve")
        self.act = Eng(self, nc.scalar, "act")
        self.pool = Eng(self, nc.gpsimd, "pool")
        self.sp = Eng(self, nc.sync, "sp")
        self.dsem = {"hw": [], "sw": []}
        for kind, n in (("hw", 20), ("sw", 8)):
            for i in range(n):
                key = "D%s%d" % (kind, i)
                if not dry:
                    self.sems[key] = nc.alloc_semaphore("d_%s%d" % (kind, i))
                self.dsem[kind].append([key, 0])
        self.dnext = {"hw": 0, "sw": 0}

    def _deps(self, Eg, reads, writes):
        for b in reads:
            Eg.wait(b.w)
        for b in writes:
            Eg.wait(b.w)
            for k, v in b.r.items():
                Eg.wait((k, v))

    def _mark(self, ev, reads, writes):
        k, v = ev
        for b in reads:
            if b.r.get(k, 0) < v:
                b.r[k] = v
        for b in writes:
            b.w = ev
            b.r = {}

    def op(self, Eg, reads, writes, fn):
        ex = [b for b in reads if b.name.startswith("ps")]
        if ex:
            reads = [b for b in reads if not b.name.startswith("ps")]
            writes = list(writes) + ex
        self._deps(Eg, reads, writes)
        Eg.cnt += 1
        if not self.dry:
            ins = fn()
            ins.then_inc(self.sems[Eg.key], 1)
        self._mark((Eg.key, Eg.cnt), reads, writes)
        self.nops += 1

    def dma(self, Q, reads, writes, fn):
        self._deps(Q, reads, writes)
        kind = "sw" if Q is self.pool else "hw"
        pool = self.dsem[kind]
        slot = pool[self.dnext[kind]]
        self.dnext[kind] = (self.dnext[kind] + 1) % len(pool)
        if slot[1] > 0:
            Q.wait((slot[0], slot[1]))
        slot[1] += 16
        if not self.dry:
            ins = fn()
            ins.then_inc(self.sems[slot[0]], 16)
        self._mark((slot[0], slot[1]), reads, writes)

    def handoff(self, olds, news):
        for nb in news:
            for ob in olds:
                if ob.w is not None:
                    k, v = ob.w
                    if nb.r.get(k, 0) < v:
                        nb.r[k] = v
                for k, v in ob.r.items():
                    if nb.r.get(k, 0) < v:
                        nb.r[k] = v

    def finish(self):
        for kind in ("hw", "sw"):
            for slot in self.dsem[kind]:
                if slot[1] > 0:
                    self.sp.wait((slot[0], slot[1]))


PCOL = {}
_o = 0
for _n, _w in [("norm_g0", 8), ("norm_g1", 8), ("b_ada0", 24), ("b_ada1", 24), ("ml_b_conv", 16), ("ml_ln_g", 16),
               ("ml_skip", 16), ("cf_b_in", 48), ("cf_b_dw", 16), ("cf_ln_g", 16), ("cf_ln_b", 16),
               ("ml_w_convT", 64), ("cf_w_dwT", 496)]:
    PCOL[_n] = _o
    _o += _w
NPCOL = _o


def build_nc():
    nc = bass.Bass("TRN2", target_bir_lowering=False)

    def din(name, shape):
        return nc.dram_tensor(name, list(shape), F32, kind="ExternalInput").ap()

    def dout(name, shape):
        return nc.dram_tensor(name, list(shape), F32, kind="ExternalOutput").ap()

    xp = din("xp", [SEQ, D]); xs = din("xs", [NS, D]); cc = din("cc", [NS + 1, D])
    sC = din("sC", [NS, H, DH, DH]); sn = din("sn", [NS, E]); sm = din("sm", [NS, H])
    smc = din("smc", [NS, 3, E]); scc = din("scc", [NS, 30, E])
    pcol_d = din("pcold", [128, NPCOL])
    w_ada = din("w_ada", [2, D, 3 * D]); b_gate = din("b_gate", [2, D])
    ml_w_in = din("ml_w_in", [D, 2 * E])
    ml_w_q = din("ml_w_q", [H, DH, DH]); ml_w_k = din("ml_w_k", [H, DH, DH]); ml_w_v = din("ml_w_v", [H, DH, DH])
    w_gates = din("w_gates", [3 * E, 8]); b_gates = din("b_gates", [8])
    ml_w_out = din("ml_w_out", [E, D])
    cf_w_in = din("cf_w_in", [D, 3 * E]); cf_w_out = din("cf_w_out", [E, D])
    final_g = din("final_g", [D])

    wcache = nc.dram_tensor("wcache", [NWC, 128, 4096], BF16, kind="Internal").ap()
    yp = dout("yp", [SEQ, D]); ys = dout("ys", [NS, D])
    Cp = dout("Cp", [H, DH, DH]); Cs = dout("Cs", [NS, H, DH, DH])
    npo = dout("npo", [H, DH]); nso = dout("nso", [NS, E])
    mpo = dout("mpo", [1, H]); mso = dout("mso", [NS, H])
    mcp = dout("mcp", [3, E]); mcs = dout("mcs", [NS, 3, E])
    ccp = dout("ccp", [30, E]); ccs = dout("ccs", [NS, 30, E])

    def sb(name, shape, dt=F32):
        return nc.alloc_sbuf_tensor(name, list(shape), dt)

    ident = sb("ident", [128, 128], BF16); identf = sb("identf", [128, 128])
    onesf = sb("onesf", [128, 128]); tri = sb("tri", [128, 128])
    maskb = sb("maskb", [128, 128]); maskT = sb("maskT", [128, 128])
    onesE = sb("onesE", [128, 128], BF16); onesb = sb("onesb", [128, 1], BF16)
    eye16 = sb("eye16", [128, 16, 16], BF16); sel16 = sb("sel16", [NS + 1, 128])
    pcol = sb("pcol", [128, NPCOL])
    fg_bc = sb("fg_bc", [128, D]); bgt_bc = sb("bgt_bc", [128, 8])
    wg = sb("wg", [128, 48, 8], BF16)
    siluT = sb("siluT", [128, 8, NS + 1], BF16)
    modT = [sb("modT%d" % l, [128, 24, NS + 1]) for l in range(2)]
    gsc = [sb("gsc%d" % l, [128, 8, NS + 1]) for l in range(2)]
    gate_t = [sb("gate_t%d" % l, [128, D]) for l in range(2)]
    xres = sb("xres", [128, 4, D])
    xn = sb("xn", [128, D], BF16)
    sm1 = sb("sm1", [128, 64])
    hT = sb("hT", [128, 8, TB], BF16)
    A1 = sb("A1", [128, 16, TB + 30], BF16)
    A2 = sb("A2", [128, 16, TB], BF16)
    yT = sb("yT", [128, 16, TB], BF16)
    szh = yT[:, 0:4, :]; qTh = yT[:, 4:8, :]; kTh = yT[:, 8:12, :]; vTh = yT[:, 12:16, :]
    mhist = sb("mhist", [128, 16, 3], BF16); chist = sb("chist", [128, 16, 30], BF16)
    ktok = sb("ktok", [128, 4, DH], BF16); vtok = sb("vtok", [128, 4, DH], BF16)
    C32 = sb("C32", [128, 16, DH])
    rn = sb("rn", [128, 2 * TB])
    Cbf = rn[:].bitcast(BF16).rearrange("p (c v) -> p c v", c=4)
    rstd_bc = rn[:, 0:TB]; nmr_bc = rn[:, TB:2 * TB]
    n32 = sb("n32", [128, 16]); nbf = sb("nbf", [128, 4], BF16); nbf2 = sb("nbf2", [128, 4], BF16)
    mprev = sb("mprev", [128, 4])
    igt = sb("igt", [128, 4, 4]); lft = sb("lft", [128, 4, 4]); gpre = sb("gpre", [128, 8])
    bcol = sb("bcol", [128, 4, 4]); btot = sb("btot", [128, 4, 4]); gcol = sb("gcol", [128, 4, 4])
    wsl = [sb("wsl%d" % i, [128, 4096], BF16) for i in range(3)]
    dg4 = [sb("dg4_%d" % i, [128, 4, 128], BF16) for i in range(2)]
    dg31 = [sb("dg31_%d" % i, [128, 11, 128], BF16) for i in range(3)]
    St = [sb("St%d" % i, [128, 128], BF16) for i in range(2)]
    ovl = sb("ovl", [128, 4096], BF16)
    WtAll = ovl[:, 0:2048].rearrange("p (a b c) -> p a b c", a=4, b=4)
    wintAll = ovl[:, 2048:4096].rearrange("p (a b c) -> p a b c", a=4, b=4)
    xn2 = sb("xn2", [128, D], BF16)
    Gf = sb("Gf", [128, 2, 16, 8]); Gbf = sb("Gbf", [128, 2, 16, 8], BF16)
    xst = sb("xst", [128, D])
    clampAll = sb("clampAll", [128, 4, 4]); wkAll = sb("wkAll", [128, 4, 4]); decayAll = sb("decayAll", [128, 4, 4])
    mtmp = sb("mtmp", [128, 32])
    qs = [sb("qs%d" % i, [128, 4, 128], BF16) for i in range(2)]
    hh = [sb("hh%d" % i, [128, DH]) for i in range(2)]
    hn = [sb("hn%d" % i, [128, DH], BF16) for i in range(2)]
    kw = [sb("kw%d" % i, [128, DH], BF16) for i in range(2)]
    csm = [sb("csm%d" % i, [128, 32]) for i in range(2)]
    sig = hn; ysq = kw; lt1 = hh
    lt2 = [q[:].rearrange("p c t -> p (c t)") for q in qs]
    ssm = sb("ssm", [NS, 64])
    qtok = ovl[0:NS, 2048:2560]
    ntok = ovl[0:NS, 0:1024].bitcast(F32); hs = ovl[0:NS, 1024:2048].bitcast(F32)
    dexp = sb("dexp", [NS, 64]); dec_bc = sb("dec_bc", [128, 64])
    kwm = [ovl[0:NS, 2560 + 512 * i:3072 + 512 * i] for i in range(2)]
    ps = [nc.alloc_psum_tensor("ps%d" % i, [128, 512], F32) for i in range(8)]
    print("sbuf bytes remaining:", nc.sbuf_bytes_remaining)
    wshared = {"descs": []}

    for dry in (True, False):
        try:
            emit(nc, dry, locals(), wshared)
        except _Stop:
            T_ = wshared["T"]
            T_.finish()
            if not dry:
                print("STOP counts:", {e.name: e.cnt for e in (T_.pe, T_.dve, T_.act, T_.pool, T_.sp)}, {k: [x[1] for x in v] for k, v in T_.dsem.items()})
    return nc


class _Stop(Exception):
    pass


def emit(nc, dry, L, ws):
    g = dict(L)
    T = Trk(nc, dry)
    ws["T"] = T
    import os as _os
    _stop = int(_os.environ.get("K_STOP", "0"))

    def stage(n):
        if _stop == n:
            raise _Stop()
    PE, DVE, ACT, POOL, SP = T.pe, T.dve, T.act, T.pool, T.sp
    V, Aeng, Pm, G = nc.vector, nc.scalar, nc.tensor, nc.gpsimd
    xp, xs, cc, sC, sn, sm, smc, scc = (g[k] for k in ["xp", "xs", "cc", "sC", "sn", "sm", "smc", "scc"])
    pcol, pcol_d = g["pcol"], g["pcol_d"]
    ps = g["ps"]
    bufs = {}

    def B(name):
        if name not in bufs:
            bufs[name] = Buf(name)
        return bufs[name]

    def op(Eg, reads, writes, fn):
        T.op(Eg, [B(r) if isinstance(r, str) else r for r in reads], [B(w) if isinstance(w, str) else w for w in writes], fn)

    def dma(Q, reads, writes, fn):
        T.dma(Q, [B(r) if isinstance(r, str) else r for r in reads], [B(w) if isinstance(w, str) else w for w in writes], fn)

    def handoff(olds, news):
        T.handoff([B(o) for o in olds], [B(n) for n in news])

    def pc(name, c, n=1):
        o = PCOL[name] + c
        return pcol[:, o:o + n]

    HL = ["szh", "qTh", "kTh", "vTh"]

    pst = {"free": list(range(8)), "i": 0}

    def bank():
        i = pst["free"][pst["i"] % len(pst["free"])]
        pst["i"] += 1
        return ps[i], B("ps%d" % i)

    def reserve(n):
        got = pst["free"][-n:]
        pst["free"] = pst["free"][:-n]
        return [(ps[i], B("ps%d" % i)) for i in got], got

    def release(got):
        pst["free"] = pst["free"] + got

    wstate = {"i": 0, "issued": 0}
    PREF = 2
    wsl = g["wsl"]

    wcache = g["wcache"]
    wc = {"map": {}, "n": 0}

    def w_issue(j):
        src, kc, ncols, key = ws["descs"][j]
        slot = j % 3
        n = kc * ncols
        dst = wsl[slot][:, 0:n].rearrange("p (k n) -> p k n", k=kc)
        if WCACHE and key is not None and key in wc["map"]:
            ci = wc["map"][key]
            dma(POOL, ["wc%d" % ci], ["wsl%d" % slot], lambda: G.dma_start(out=wsl[slot][:, 0:n], in_=wcache[ci, :, 0:n]))
            return
        dma(POOL, [], ["wsl%d" % slot], lambda: G.dma_start(out=dst, in_=src.rearrange("(k p) n -> p k n", p=128)))
        if WCACHE and key is not None and ws["uses"].get(key, 0) > 1 and wc["n"] < NWC:
            ci = wc["n"]
            wc["n"] += 1
            wc["map"][key] = ci
            dma(POOL, ["wsl%d" % slot], ["wc%d" % ci], lambda: G.dma_start(out=wcache[ci, :, 0:n], in_=wsl[slot][:, 0:n]))

    def wget(src, kc, ncols, key=None):
        i = wstate["i"]
        wstate["i"] += 1
        slot = i % 3
        view = wsl[slot][:, 0:kc * ncols].rearrange("p (k n) -> p k n", k=kc)
        if dry:
            ws["descs"].append((src, kc, ncols, key))
            if key is not None:
                ws.setdefault("uses", {})
                ws["uses"][key] = ws["uses"].get(key, 0) + 1
            return view, B("wsl%d" % slot)
        while wstate["issued"] < min(len(ws["descs"]), i + PREF + 1):
            w_issue(wstate["issued"])
            wstate["issued"] += 1
        return view, B("wsl%d" % slot)

    ident, identf, onesf, tri, maskb, maskT, onesE, onesb, eye16, sel16 = (
        g[k] for k in ["ident", "identf", "onesf", "tri", "maskb", "maskT", "onesE", "onesb", "eye16", "sel16"])
    sm1 = g["sm1"]

    op(POOL, [], ["identf"], lambda: G.memset(identf[:], 0.0))
    op(POOL, [], ["identf"], lambda: G.affine_select(out=identf[:], in_=identf[:], pattern=[[-1, 128]], compare_op=ALU.not_equal,
                                                     fill=1.0, base=0, channel_multiplier=1))
    op(DVE, ["identf"], ["ident"], lambda: V.tensor_copy(ident[:], identf[:]))
    op(POOL, [], ["onesf"], lambda: G.memset(onesf[:], 1.0))
    op(POOL, [], ["onesE"], lambda: G.memset(onesE[:], 1.0 / E))
    op(POOL, [], ["onesb"], lambda: G.memset(onesb[:], 1.0))
    op(POOL, [], ["tri"], lambda: G.memset(tri[:], 1.0))
    op(POOL, [], ["tri"], lambda: G.affine_select(out=tri[:], in_=tri[:], pattern=[[1, 128]], compare_op=ALU.is_ge, fill=0.0,
                                                  base=0, channel_multiplier=-1))
    op(POOL, [], ["maskb"], lambda: G.memset(maskb[:], 0.0))
    op(POOL, [], ["maskb"], lambda: G.affine_select(out=maskb[:], in_=maskb[:], pattern=[[-1, 128]], compare_op=ALU.is_ge, fill=-BIG,
                                                    base=0, channel_multiplier=1))
    op(POOL, [], ["maskT"], lambda: G.memset(maskT[:], 0.0))
    op(POOL, [], ["maskT"], lambda: G.affine_select(out=maskT[:], in_=maskT[:], pattern=[[1, 128]], compare_op=ALU.is_ge, fill=BIG,
                                                    base=0, channel_multiplier=-1))
    op(POOL, [], ["eye16"], lambda: G.memset(eye16[:], 1.0))
    op(POOL, [], ["eye16"], lambda: G.affine_select(out=eye16[:], in_=eye16[:], pattern=[[1, 16], [-1, 16]], compare_op=ALU.is_equal,
                                                    fill=0.0, base=0, channel_multiplier=0))
    op(POOL, [], ["sel16"], lambda: G.memset(sel16[:], 1.0))
    op(POOL, [], ["sel16"], lambda: G.affine_select(out=sel16[:], in_=sel16[:], pattern=[[0, 128]], compare_op=ALU.is_ge, fill=0.0,
                                                    base=-NS, channel_multiplier=1))
    dma(SP, [], ["pcol"], lambda: nc.sync.dma_start(out=pcol[:], in_=pcol_d))
    fg_bc, bgt_bc, wg = g["fg_bc"], g["bgt_bc"], g["wg"]
    dma(SP, [], ["fg_bc"], lambda: nc.sync.dma_start(out=fg_bc[:], in_=g["final_g"].partition_broadcast(128)))
    dma(SP, [], ["bgt_bc"], lambda: nc.sync.dma_start(out=bgt_bc[:], in_=g["b_gates"].partition_broadcast(128)))
    dma(POOL, [], ["wg"], lambda: G.dma_start(out=wg[:], in_=g["w_gates"].rearrange("(k p) n -> p k n", p=128)))

    siluT, modT, gsc, gate_t = g["siluT"], g["modT"], g["gsc"], g["gate_t"]
    xres, xn = g["xres"], g["xn"]
    R = NS + 1
    dma(SP, [], ["xres"], lambda: nc.sync.dma_start(out=xres[0:R, 0, :], in_=cc))
    op(ACT, ["xres"], ["xres"], lambda: Aeng.activation(out=xres[0:R, 1, :], in_=xres[0:R, 0, :], func=AF.Silu))
    pt, pb = bank()
    for c in range(8):
        op(PE, ["xres", "identf"], [pb], lambda c=c, pt=pt: Pm.transpose(pt[:, c * 32:c * 32 + R], xres[0:R, 1, c * 128:(c + 1) * 128],
                                                                       identf[0:R, 0:R]))
    op(DVE, [pb], ["siluT"], lambda pt=pt: V.tensor_copy(siluT[:], pt[:, 0:256].rearrange("p (c r) -> p c r", c=8)[:, :, 0:R]))
    bg_rows = xres[0:R, 2, :]
    for l in range(2):
        dma(SP, ["xres"], ["xres"], lambda l=l: nc.sync.dma_start(out=bg_rows, in_=g["b_gate"][l].partition_broadcast(R)))
        for p in range(6):
            wv, wb = wget(g["w_ada"][l][:, p * 512:(p + 1) * 512], 8, 512)
            pt, pb = bank()

            def mm(wv=wv, pt=pt):
                ins = None
                for fc in range(4):
                    for kc in range(8):
                        ins = Pm.matmul(pt[:, fc * 32:fc * 32 + R], lhsT=wv[:, kc, fc * 128:(fc + 1) * 128], rhs=siluT[:, kc, :],
                                        start=(kc == 0), stop=(kc == 7))
                return ins
            op(PE, [wb, "siluT"], [pb], mm)
            for fc in range(4):
                f = 4 * p + fc
                op(ACT, [pb, "pcol"], ["modT%d" % l], lambda f=f, fc=fc, pt=pt, l=l: Aeng.activation(
                    out=modT[l][:, f, :], in_=pt[:, fc * 32:fc * 32 + R], func=AF.Identity, bias=pc("b_ada%d" % l, f)))
            if p >= 4:
                pt2, pb2 = bank()

                def mm2(wv=wv, pt2=pt2):
                    ins = None
                    for kc in range(8):
                        ins = Pm.matmul(pt2[0:R, :], lhsT=siluT[:, kc, :], rhs=wv[:, kc, :], start=(kc == 0), stop=(kc == 7))
                    return ins
                op(PE, [wb, "siluT"], [pb2], mm2)
                cs = slice((p - 4) * 512, (p - 3) * 512)
                op(DVE, [pb2, "xres"], ["gate_t%d" % l], lambda pt2=pt2, cs=cs, l=l: V.tensor_tensor(
                    out=gate_t[l][0:R, cs], in0=pt2[0:R, :], in1=bg_rows[:, cs], op=ALU.add))
        op(DVE, ["modT%d" % l], ["gsc%d" % l], lambda l=l: V.tensor_scalar(
            out=gsc[l][:], in0=modT[l][:, 8:16, :], scalar1=1.0, scalar2=None, op0=ALU.add))
        op(DVE, ["gsc%d" % l, "pcol"], ["gsc%d" % l], lambda l=l: V.tensor_tensor(
            out=gsc[l][:], in0=gsc[l][:], in1=pc("norm_g%d" % l, 0, 8).unsqueeze(2).to_broadcast([128, 8, R]), op=ALU.mult))

    Gf, Gbf = g["Gf"], g["Gbf"]
    WTs = g["yT"]
    for hq in range(H):
        for wi, wd in enumerate([g["ml_w_q"], g["ml_w_k"], g["ml_w_v"]]):
            wv, wb = wget(wd[hq], 4, 512, key=("qkv", wi, hq))
            for ec in range(4):
                pt, pb = bank()
                ptb = pt[:].bitcast(BF16)
                for kc in range(4):
                    op(PE, [wb, "ident"], [pb], lambda kc=kc, ec=ec, ptb=ptb, wv=wv: Pm.transpose(
                        ptb[:, kc * 128:(kc + 1) * 128], wv[:, kc, ec * 128:(ec + 1) * 128], ident[:]))
                op(DVE, [pb], ["WTs"], lambda ec=ec, ptb=ptb: V.tensor_copy(WTs[:, ec, :], ptb[:, 0:512]))
            pt2, pb2 = bank()

            def mmf(pt2=pt2, wi=wi, hq=hq):
                ins = None
                for kc in range(4):
                    for ec in range(4):
                        ins = Pm.matmul(pt2[:, kc * 8:(kc + 1) * 8], lhsT=WTs[:, ec, kc * 128:(kc + 1) * 128],
                                        rhs=wg[:, wi * 16 + hq * 4 + ec, :], start=(ec == 0), stop=(ec == 3))
                return ins
            op(PE, ["WTs", "wg"], [pb2], mmf)
            dstG = Gf[:, 1 if wi == 2 else 0, 4 * hq:4 * hq + 4, :]
            src = pt2[:, 0:32].rearrange("p (k g) -> p k g", k=4)
            if wi == 1:
                op(DVE, [pb2, "Gf"], ["Gf"], lambda dstG=dstG, src=src: V.tensor_tensor(out=dstG, in0=dstG, in1=src, op=ALU.add))
            else:
                op(DVE, [pb2], ["Gf"], lambda dstG=dstG, src=src: V.tensor_copy(dstG, src))
    op(DVE, ["Gf"], ["Gbf"], lambda: V.tensor_copy(Gbf[:], Gf[:]))
    handoff(["WTs"], ["yT"] + HL)

    stage(1)
    hT, A1, A2, yT = g["hT"], g["A1"], g["A2"], g["yT"]
    hh, hn, kw, qs, csm = g["hh"], g["hn"], g["kw"], g["qs"], g["csm"]
    sig, ysq, lt1, lt2 = g["sig"], g["ysq"], g["lt1"], g["lt2"]
    rstd_bc, nmr_bc = g["rstd_bc"], g["nmr_bc"]

    xnb = [g["xn"], g["xn2"]]
    rr = {"n": 0}

    def rms_rows(src, nt, srcbuf="xres"):
        par = rr["n"] % 2
        rr["n"] += 1
        sn_ = "sm1_%d" % par
        c0 = par * 16
        st = sm1[0:nt, c0:c0 + 16]
        op(DVE, [srcbuf], [sn_], lambda: V.bn_stats(out=st[:, 0:6], in_=src[:, 0:512]))
        op(DVE, [srcbuf, sn_], [sn_], lambda: V.bn_stats(out=st[:, 6:12], in_=src[:, 512:1024]))
        op(DVE, [sn_], [sn_], lambda: V.bn_aggr(out=st[:, 12:14], in_=st[:, 0:12]))
        op(DVE, [sn_], [sn_], lambda: V.scalar_tensor_tensor(out=st[:, 14:15], in0=st[:, 12:13], scalar=st[:, 12:13], in1=st[:, 13:14],
                                                            op0=ALU.mult, op1=ALU.add))
        op(ACT, [sn_], [sn_], lambda: Aeng.activation(out=st[:, 15:16], in_=st[:, 14:15], func=AF.Ln, bias=EPS))
        op(ACT, [sn_], [sn_], lambda: Aeng.activation(out=st[:, 15:16], in_=st[:, 15:16], func=AF.Exp, scale=-0.5))
        return st[:, 15:16], sn_

    def norm_mod(l, ntiles, nt, is_s, stage_src=None):
        for tt in range(ntiles):
            if stage_src is not None:
                dma(SP, [], ["xst"], lambda tt=tt: nc.sync.dma_start(out=g["xst"][:, :], in_=stage_src(tt)))
                srcx, sbn = g["xst"][0:nt, :], "xst"
            else:
                srcx, sbn = xres[0:nt, tt, :], "xres"
            rs, sn_ = rms_rows(srcx, nt, sbn)
            xb = xnb[tt % 2]
            xbn = "xn%d" % (tt % 2)
            op(ACT, [sbn, sn_], [xbn], lambda xb=xb, rs=rs, srcx=srcx: Aeng.activation(out=xb[0:nt, :], in_=srcx, func=AF.Copy, scale=rs))
            pt, pb = bank()
            ptb = pt[:].bitcast(BF16).rearrange("p (c t) -> p c t", c=8)
            for c in range(8):
                op(PE, [xbn, "ident"], [pb], lambda c=c, ptb=ptb, xb=xb: Pm.transpose(ptb[:, c, 0:nt], xb[0:nt, c * 128:(c + 1) * 128],
                                                                                   ident[0:nt, 0:nt]))
            for c in range(8):
                dst = hT[:, c, tt * 128:tt * 128 + nt]
                if is_s:
                    tmp = sm1[:, 32:32 + NS]
                    op(DVE, [pb, "gsc%d" % l], ["sm1b"], lambda c=c, ptb=ptb, tmp=tmp: V.tensor_tensor(
                        out=tmp, in0=ptb[:, c, 0:nt], in1=gsc[l][:, c, 0:NS], op=ALU.mult))
                    op(DVE, ["sm1b", "modT%d" % l], ["hT"], lambda c=c, dst=dst, tmp=tmp: V.tensor_tensor(
                        out=dst, in0=tmp, in1=modT[l][:, c, 0:NS], op=ALU.add))
                else:
                    op(DVE, [pb, "gsc%d" % l, "modT%d" % l], ["hT"], lambda c=c, dst=dst, ptb=ptb: V.tensor_scalar(
                        out=dst, in0=ptb[:, c, 0:nt], scalar1=gsc[l][:, c, NS:NS + 1], scalar2=modT[l][:, c, NS:NS + 1],
                        op0=ALU.mult, op1=ALU.add))

    def fm_group(wv, wb, srcf, src_bufs, KC, N, nfc, evac):
        for fc in range(nfc):
            pt, pb = bank()

            def mm(fc=fc, pt=pt):
                ins = None
                for kc in range(KC):
                    ins = Pm.matmul(pt[:, 0:N], lhsT=wv[:, kc, fc * 128:(fc + 1) * 128], rhs=srcf(kc), start=(kc == 0), stop=(kc == KC - 1))
                return ins
            op(PE, [wb] + src_bufs, [pb], mm)
            evac(fc, pt, pb)

    def tm_group(wv, wb, lhs_f, src_bufs, KC, nt, ncols, evac):
        pt, pb = bank()

        def mm():
            ins = None
            for kc in range(KC):
                ins = Pm.matmul(pt[0:nt, 0:ncols], lhsT=lhs_f(kc), rhs=wv[:, kc, 0:ncols], start=(kc == 0), stop=(kc == KC - 1))
            return ins
        op(PE, [wb] + src_bufs, [pb], mm)
        evac(pt, pb)

    def rows_out(srcf, src_bufs, Rr, dstf):
        for q in range(4):
            pt, pb = bank()
            ptb = pt[:].bitcast(BF16)
            for cc_ in range(4):
                c = q * 4 + cc_
                op(PE, src_bufs + ["ident"], [pb], lambda c=c, cc_=cc_, ptb=ptb: Pm.transpose(ptb[0:Rr, cc_ * 128:(cc_ + 1) * 128], srcf(c),
                                                                                         ident[:, :]))
            st = hh[q % 2]
            op(DVE, [pb], ["hh%d" % (q % 2)], lambda st=st, ptb=ptb: V.tensor_copy(st[0:Rr, :], ptb[0:Rr, 0:512]))
            dma(SP, ["hh%d" % (q % 2)], [], lambda st=st, q=q: nc.sync.dma_start(out=dstf(q), in_=st[0:Rr, :]))

    def outproj_resid(l, wdram, ntiles, nt):
        for hf in range(2):
            banks, got = reserve(ntiles)
            for kh in range(2):
                wv, wb = wget(wdram[kh * 1024:(kh + 1) * 1024, hf * 512:(hf + 1) * 512], 8, 512, key=("wout", l, kh, hf))
                for tt in range(ntiles):
                    pt, pb = banks[tt]

                    def mm(tt=tt, pt=pt, wv=wv, kh=kh):
                        ins = None
                        for kc in range(8):
                            ins = Pm.matmul(pt[0:nt, :], lhsT=A2[:, kh * 8 + kc, tt * 128:tt * 128 + nt], rhs=wv[:, kc, :],
                                            start=(kh == 0 and kc == 0), stop=(kh == 1 and kc == 7))
                        return ins
                    op(PE, [wb, "A2"], [pb], mm)
            for tt in range(ntiles):
                pt, pb = banks[tt]
                gt = gate_t[l][0:nt, hf * 512:(hf + 1) * 512]
                xr = xres[0:nt, tt, hf * 512:(hf + 1) * 512]
                lt = hh[tt % 2]
                op(DVE, [pb, "gate_t%d" % l], ["hh%d" % (tt % 2)], lambda pt=pt, gt=gt, lt=lt: V.tensor_tensor(
                    out=lt[0:nt, :], in0=pt[0:nt, :], in1=gt, op=ALU.mult))
                op(DVE, ["hh%d" % (tt % 2), "xres"], ["xres"], lambda xr=xr, lt=lt: V.tensor_tensor(out=xr, in0=xr, in1=lt[0:nt, :], op=ALU.add))
            release(got)

    szh, qTh, kTh, vTh, ktok, vtok, qtok = (g[k] for k in ["szh", "qTh", "kTh", "vTh", "ktok", "vtok", "qtok"])
    C32, Cbf, n32, nbf, mprev = (g[k] for k in ["C32", "Cbf", "n32", "nbf", "mprev"])
    igt, lft, gpre, bcol, btot, gcol = (g[k] for k in ["igt", "lft", "gpre", "bcol", "btot", "gcol"])
    mhist, chist = g["mhist"], g["chist"]

    def layer0_front(ntiles, nt, is_s, stride, skip_norm=False):
        N = ntiles * nt
        if not skip_norm:
            norm_mod(0, ntiles, nt, is_s)
        newc = slice(3 * stride, 3 * stride + N)
        for p in range(4):
            wv, wb = wget(g["ml_w_in"][:, p * 512:(p + 1) * 512], 8, 512, key=("win", p))

            def ev(fc, pt, pb, p=p):
                op(ACT, [pb], ["A1"], lambda: Aeng.activation(out=A1[:, 4 * p + fc, newc], in_=pt[:, 0:N], func=AF.Copy))
            fm_group(wv, wb, lambda kc: hT[:, kc, 0:N], ["hT"], 8, N, 4, ev)
        dg4 = g["dg4"]
        for c in range(16):
            dg = dg4[c % 2]
            op(POOL, ["ident", "pcol"], ["dg4_%d" % (c % 2)], lambda c=c, dg=dg: G.tensor_tensor(
                out=dg[:], in0=ident[:].unsqueeze(1).to_broadcast([128, 4, 128]),
                in1=pc("ml_w_convT", 4 * c, 4).unsqueeze(2).to_broadcast([128, 4, 128]), op=ALU.mult))
            pt, pb = bank()

            def mm(c=c, dg=dg, pt=pt):
                ins = None
                for j in range(4):
                    ins = Pm.matmul(pt[:, 0:N], lhsT=dg[:, j, :], rhs=A1[:, c, j * stride:j * stride + N], start=(j == 0), stop=(j == 3))
                return ins
            op(PE, ["dg4_%d" % (c % 2), "A1"], [pb], mm)
            op(ACT, [pb, "pcol"], ["A2"], lambda c=c, pt=pt: Aeng.activation(out=A2[:, c, 0:N], in_=pt[:, 0:N], func=AF.Silu,
                                                                           bias=pc("ml_b_conv", c)))
        handoff(["yT"], HL)
        gbanks, ggot = reserve(ntiles)
        for tt in range(ntiles):
            gpt, gpb = gbanks[tt]

            def mmgt(tt=tt, gpt=gpt):
                ins = None
                for ch in range(16):
                    ins = Pm.matmul(gpt[0:nt, 0:8], lhsT=A2[:, ch, tt * 128:tt * 128 + nt], rhs=Gbf[:, 0, ch, :], start=(ch == 0), stop=False)
                for ch in range(16):
                    c0 = 3 * stride + tt * 128
                    ins = Pm.matmul(gpt[0:nt, 0:8], lhsT=A1[:, ch, c0:c0 + nt], rhs=Gbf[:, 1, ch, :], start=False, stop=(ch == 15))
                return ins
            op(PE, ["A2", "A1", "Gbf"], [gpb], mmgt)
        for tt in range(ntiles):
            gpt, gpb = gbanks[tt]
            op(DVE, [gpb, "bgt_bc"], ["gpre"], lambda tt=tt, gpt=gpt: V.tensor_tensor(out=gpre[0:nt, :], in0=gpt[0:nt, 0:8],
                                                                                  in1=bgt_bc[0:nt, :], op=ALU.add))
            op(DVE, ["gpre"], ["igt"], lambda tt=tt: V.tensor_copy(igt[0:nt, tt, :], gpre[0:nt, 0:4]))
            op(ACT, ["gpre"], ["gpre"], lambda: Aeng.activation(out=gpre[0:nt, 4:8], in_=gpre[0:nt, 4:8], func=AF.Exp, scale=-1.0))
            op(ACT, ["gpre"], ["gpre"], lambda: Aeng.activation(out=gpre[0:nt, 4:8], in_=gpre[0:nt, 4:8], func=AF.Ln, bias=1.0))
            op(DVE, ["gpre"], ["lft"], lambda tt=tt: V.tensor_scalar(out=lft[0:nt, tt, :], in0=gpre[0:nt, 4:8], scalar1=-1.0, scalar2=None,
                                                                   op0=ALU.mult))
        release(ggot)

    def head_proj(h, ntiles, nt, N, stride, is_s, between=None):
        bt = between or (lambda i: None)
        wv, wb = wget(g["ml_w_in"][:, E + h * 512:E + (h + 1) * 512], 8, 512, key=("win", 4 + h))

        def evz(fc, pt, pb):
            op(ACT, [pb], ["szh"], lambda: Aeng.activation(out=szh[:, fc, 0:N], in_=pt[:, 0:N], func=AF.Silu))
        bt(0)
        fm_group(wv, wb, lambda kc: hT[:, kc, 0:N], ["hT"], 8, N, 4, evz)
        bt(1)
        wv, wb = wget(g["ml_w_q"][h], 4, 512, key=("qkv", 0, h))

        def evq(fc, pt, pb):
            op(DVE, [pb], ["qTh"], lambda: V.tensor_copy(qTh[:, fc, 0:N], pt[:, 0:N]))
        fm_group(wv, wb, lambda kc: A2[:, 4 * h + kc, 0:N], ["A2"], 4, N, 4, evq)
        if is_s:
            def evqt(pt, pb):
                op(DVE, [pb], ["qtok"], lambda: V.tensor_copy(qtok[0:nt, :], pt[0:nt, :]))
            tm_group(wv, wb, lambda kc: A2[:, 4 * h + kc, 0:nt], ["A2"], 4, nt, 512, evqt)
        bt(2)
        wv, wb = wget(g["ml_w_k"][h], 4, 512, key=("qkv", 1, h))
        ksc = float(DH) ** -0.5

        def evk(fc, pt, pb):
            op(ACT, [pb], ["kTh"], lambda: Aeng.activation(out=kTh[:, fc, 0:N], in_=pt[:, 0:N], func=AF.Copy, scale=ksc))
        fm_group(wv, wb, lambda kc: A2[:, 4 * h + kc, 0:N], ["A2"], 4, N, 4, evk)
        for tt in range(ntiles):
            def evkt(pt, pb, tt=tt):
                dst = ktok[0:nt, h, :] if is_s else ktok[0:nt, tt, :]
                op(ACT, [pb], ["ktok"], lambda: Aeng.activation(out=dst, in_=pt[0:nt, :], func=AF.Copy, scale=ksc))
            tm_group(wv, wb, lambda kc, tt=tt: A2[:, 4 * h + kc, tt * 128:tt * 128 + nt], ["A2"], 4, nt, 512, evkt)
        bt(3)
        wv, wb = wget(g["ml_w_v"][h], 4, 512, key=("qkv", 2, h))
        for tt in range(ntiles):
            def evvt(pt, pb, tt=tt):
                dst = vtok[0:nt, h, :] if is_s else vtok[0:nt, tt, :]
                op(DVE, [pb], ["vtok"], lambda: V.tensor_copy(dst, pt[0:nt, :]))
            tm_group(wv, wb, lambda kc, tt=tt: A1[:, 4 * h + kc, 3 * stride + tt * 128:3 * stride + tt * 128 + nt], ["A1"], 4, nt, 512, evvt)

    def scale_xc_skip(h, N):
        for c in range(4):
            ch = 4 * h + c
            op(DVE, ["A2", "pcol"], ["A2"], lambda ch=ch: V.tensor_scalar(out=A2[:, ch, 0:N], in0=A2[:, ch, 0:N], scalar1=pc("ml_skip", ch),
                                                                      scalar2=None, op0=ALU.mult))

    def gate_prep(h, N):
        for c in range(4):
            ch = 4 * h + c
            op(DVE, ["A2", "pcol", "szh"], ["A2"], lambda c=c, ch=ch: V.scalar_tensor_tensor(
                out=A2[:, ch, 0:N], in0=A2[:, ch, 0:N], scalar=pc("ml_skip", ch), in1=szh[:, c, 0:N], op0=ALU.mult, op1=ALU.mult))
        for c in range(4):
            ch = 4 * h + c
            op(DVE, ["szh", "pcol"], ["szh"], lambda c=c, ch=ch: V.tensor_scalar(
                out=szh[:, c, 0:N], in0=szh[:, c, 0:N], scalar1=pc("ml_ln_g", ch), scalar2=None, op0=ALU.mult))

    def ln_rows(par, nt, src_hh, src_buf, dst_hn):
        cs_ = csm[par]
        cn = "csm%d" % par
        op(DVE, [src_buf], [cn], lambda: V.bn_stats(out=cs_[0:nt, 0:6], in_=src_hh))
        op(DVE, [cn], [cn], lambda: V.bn_aggr(out=cs_[0:nt, 6:8], in_=cs_[0:nt, 0:6]))
        op(ACT, [cn], [cn], lambda: Aeng.activation(out=cs_[0:nt, 8:9], in_=cs_[0:nt, 7:8], func=AF.Ln, bias=EPS))
        op(ACT, [cn], [cn], lambda: Aeng.activation(out=cs_[0:nt, 8:9], in_=cs_[0:nt, 8:9], func=AF.Exp, scale=-0.5))
        op(DVE, [cn], [cn], lambda: V.tensor_scalar(out=cs_[0:nt, 9:10], in0=cs_[0:nt, 6:7], scalar1=cs_[0:nt, 8:9], scalar2=-1.0,
                                                   op0=ALU.mult, op1=ALU.mult))
        op(ACT, [src_buf, cn], ["hn%d" % par], lambda: Aeng.activation(out=dst_hn, in_=src_hh, func=AF.Identity,
                                                                      scale=cs_[0:nt, 8:9], bias=cs_[0:nt, 9:10]))

    WtAll, wintAll, clampAll, wkAll, decayAll, mtmp, St = (g[k] for k in ["WtAll", "wintAll", "clampAll", "wkAll", "decayAll", "mtmp", "St"])

    def psb(i):
        return ps[i], B("ps%d" % i)

    def mchain(tt):
        dgG = hh[0][:].rearrange("p (h s) -> p h s", h=4)
        dgM = hh[1][:].rearrange("p (h s) -> p h s", h=4)
        pt, pb = bank()

        def mm(pt=pt):
            Pm.matmul(pt[:, 0:4], lhsT=tri[:], rhs=lft[:, tt, :], start=True, stop=True)
            return Pm.matmul(pt[:, 4:8], lhsT=onesf[:], rhs=lft[:, tt, :], start=True, stop=True)
        op(PE, ["tri", "onesf", "lft"], [pb], mm)
        op(DVE, [pb], ["bcol"], lambda pt=pt: V.tensor_copy(bcol[:, tt, :], pt[:, 0:4]))
        op(DVE, [pb], ["btot"], lambda pt=pt: V.tensor_copy(btot[:, tt, :], pt[:, 4:8]))
        op(DVE, ["igt", "bcol"], ["gcol"], lambda: V.tensor_tensor(out=gcol[:, tt, :], in0=igt[:, tt, :], in1=bcol[:, tt, :], op=ALU.subtract))
        op(DVE, ["identf", "gcol"], ["hh0"], lambda: V.tensor_tensor(
            out=dgG, in0=identf[:].unsqueeze(1).to_broadcast([128, 4, 128]), in1=gcol[:, tt, :].unsqueeze(2).to_broadcast([128, 4, 128]),
            op=ALU.mult))
        ptG, pbG = bank()

        def mmG(ptG=ptG):
            Pm.matmul(ptG[:, :], lhsT=onesf[:], rhs=hh[0][:], start=True, stop=False)
            ins = None
            for hq in range(4):
                ins = Pm.matmul(ptG[:, hq * 128:(hq + 1) * 128], lhsT=identf[:], rhs=maskb[:], start=False, stop=(hq == 3))
            return ins
        op(PE, ["hh0", "onesf", "identf", "maskb"], [pbG], mmG)
        op(DVE, [pbG], ["mtmp"], lambda ptG=ptG: V.tensor_reduce(out=mtmp[:, 0:4], in_=ptG[:, :].rearrange("p (h s) -> p h s", h=4), axis=AX.X,
                                                              op=ALU.max))
        op(DVE, ["mtmp", "mprev"], ["mtmp"], lambda: V.tensor_tensor(out=mtmp[:, 4:8], in0=mtmp[:, 0:4], in1=mprev[:], op=ALU.max))
        op(DVE, ["identf", "mtmp"], ["hh1"], lambda: V.tensor_tensor(
            out=dgM, in0=identf[:].unsqueeze(1).to_broadcast([128, 4, 128]), in1=mtmp[:, 4:8].unsqueeze(2).to_broadcast([128, 4, 128]),
            op=ALU.mult))
        ptA, pbA = bank()
        ptB, pbB = bank()
        op(PE, ["hh1", "onesf"], [pbA], lambda ptA=ptA: Pm.matmul(ptA[:, :], lhsT=onesf[:], rhs=hh[1][:], start=True, stop=True))

        def mmB(ptB=ptB):
            Pm.matmul(ptB[:, :], lhsT=onesf[:], rhs=hh[1][:], start=True, stop=False)
            ins = None
            for hq in range(4):
                ins = Pm.matmul(ptB[:, hq * 128:(hq + 1) * 128], lhsT=identf[:], rhs=maskT[:], start=False, stop=(hq == 3))
            return ins
        op(PE, ["hh1", "onesf", "identf", "maskT"], [pbB], mmB)
        for hq in range(4):
            op(ACT, [pbB, "gcol"], ["WtAll"], lambda hq=hq, ptB=ptB: Aeng.activation(
                out=WtAll[:, tt, hq, :], in_=ptB[:, hq * 128:(hq + 1) * 128], func=AF.Exp, scale=-1.0, bias=gcol[:, tt, hq:hq + 1]))
        for hq in range(4):
            op(ACT, [pbA, "mprev"], ["wintAll"], lambda hq=hq, ptA=ptA: Aeng.activation(
                out=wintAll[:, tt, hq, :], in_=ptA[:, hq * 128:(hq + 1) * 128], func=AF.Exp, scale=-1.0, bias=mprev[:, hq:hq + 1]))
        op(DVE, [pbA], ["mtmp"], lambda ptA=ptA: V.tensor_copy(mtmp[:, 8:12], ptA[:, :].rearrange("p (h s) -> p h s", h=4)[:, :, 127]))
        op(DVE, ["mtmp", "bcol"], ["mtmp"], lambda: V.tensor_tensor(out=mtmp[:, 12:16], in0=mtmp[:, 4:8], in1=bcol[:, tt, :], op=ALU.add))
        op(ACT, ["mtmp"], ["clampAll"], lambda: Aeng.activation(out=clampAll[:, tt, :], in_=mtmp[:, 12:16], func=AF.Exp, scale=-1.0))
        op(DVE, ["mtmp", "gcol"], ["mtmp"], lambda: V.tensor_tensor(out=mtmp[:, 16:20], in0=gcol[:, tt, :], in1=mtmp[:, 8:12], op=ALU.subtract))
        op(ACT, ["mtmp"], ["wkAll"], lambda: Aeng.activation(out=wkAll[:, tt, :], in_=mtmp[:, 16:20], func=AF.Exp))
        op(DVE, ["mtmp", "mprev"], ["mtmp"], lambda: V.tensor_tensor(out=mtmp[:, 20:24], in0=mprev[:], in1=mtmp[:, 8:12], op=ALU.subtract))
        op(ACT, ["mtmp"], ["decayAll"], lambda: Aeng.activation(out=decayAll[:, tt, :], in_=mtmp[:, 20:24], func=AF.Exp))
        op(DVE, ["btot", "mtmp", "mprev"], ["mprev"], lambda: V.tensor_tensor(out=mprev[:], in0=btot[:, tt, :], in1=mtmp[:, 8:12], op=ALU.add))

    def prompt_cell(h, ntiles):
        Chs = ["C32_%d_%d" % (h, c) for c in range(4)]
        Cb = [Cbf, vTh]
        Cbn = ["rn", "vTh"]
        nb = [nbf, g["nbf2"]]
        nbn = ["nbf", "nbf2"]
        op(ACT, Chs, [Cbn[1]], lambda: Aeng.activation(out=Cb[1], in_=C32[:, 4 * h:4 * h + 4, :], func=AF.Copy))
        op(DVE, ["n32"], [nbn[1]], lambda: V.tensor_copy(nb[1][:], n32[:, 4 * h:4 * h + 4]))

        def F(tt):
            par = tt % 2
            tk = slice(tt * 128, (tt + 1) * 128)
            ptS, pbS = psb(2 + par)
            sn_, qn_ = "St%d" % par, "qs%d" % par

            def mmS():
                ins = None
                for c in range(4):
                    ins = Pm.matmul(ptS[:, 0:128], lhsT=kTh[:, c, tk], rhs=qTh[:, c, tk], start=(c == 0), stop=(c == 3))
                return ins
            op(PE, ["kTh", "qTh"], [pbS], mmS)
            op(DVE, [pbS, "WtAll"], [sn_], lambda: V.tensor_tensor(out=St[par][:], in0=ptS[:, 0:128], in1=WtAll[:, tt, h, :], op=ALU.mult))
            op(DVE, ["qTh", "wintAll"], [qn_], lambda: V.tensor_tensor(
                out=qs[par][:], in0=qTh[:, :, tk], in1=wintAll[:, tt, h, :].unsqueeze(1).to_broadcast([128, 4, 128]), op=ALU.mult))

        def N(tt):
            par = tt % 2
            prv = (tt + 1) % 2
            ptS, pbS = psb(2 + par)
            ptN, pbN = psb(par)
            sn_, qn_ = "St%d" % par, "qs%d" % par

            def mmN():
                Pm.matmul(ptN[:, :], lhsT=St[par][:], rhs=vtok[:, tt, :], start=True, stop=False)
                ins = None
                for c in range(4):
                    ins = Pm.matmul(ptN[:, :], lhsT=qs[par][:, c, :], rhs=Cb[prv][:, c, :], start=False, stop=(c == 3))
                return ins
            op(PE, [sn_, "vtok", qn_, Cbn[prv]], [pbN], mmN)

            def mmD():
                Pm.matmul(ptS[:, 136:137], lhsT=St[par][:], rhs=onesb[:, 0:1], start=True, stop=False)
                ins = None
                for c in range(4):
                    ins = Pm.matmul(ptS[:, 136:137], lhsT=qs[par][:, c, :], rhs=nb[prv][:, c:c + 1], start=False, stop=(c == 3))
                return ins
            op(PE, [sn_, "onesb", qn_, nbn[prv]], [pbS], mmD)

        def U(tt):
            par = tt % 2
            ptS, pbS = psb(2 + par)
            kwn = "kw%d" % par
            op(ACT, ["ktok", "wkAll"], [kwn], lambda: Aeng.activation(out=kw[par][:], in_=ktok[:, tt, :], func=AF.Copy,
                                                                     scale=wkAll[:, tt, h:h + 1]))
            for c in range(4):
                ptC, pbC = psb(4 + (tt * 4 + c) % 3)
                op(PE, [kwn, "vtok"], [pbC], lambda c=c, ptC=ptC: Pm.matmul(ptC[:, :], lhsT=kw[par][:, c * 128:(c + 1) * 128],
                                                                         rhs=vtok[:, tt, :], start=True, stop=True))
                op(DVE, [pbC, "decayAll", Chs[c]], [Chs[c]], lambda c=c, ptC=ptC: V.scalar_tensor_tensor(
                    out=C32[:, 4 * h + c, :], in0=C32[:, 4 * h + c, :], scalar=decayAll[:, tt, h:h + 1], in1=ptC[:, :], op0=ALU.mult,
                    op1=ALU.add))
                op(ACT, [Chs[c]], [Cbn[par]], lambda c=c: Aeng.activation(out=Cb[par][:, c, :], in_=C32[:, 4 * h + c, :], func=AF.Copy))

            def mmn():
                ins = None
                for c in range(4):
                    ins = Pm.matmul(ptS[:, 128 + c:129 + c], lhsT=kw[par][:, c * 128:(c + 1) * 128], rhs=onesb[:, 0:1], start=True, stop=True)
                return ins
            op(PE, [kwn, "onesb"], [pbS], mmn)
            op(DVE, [pbS, "decayAll", "n32"], ["n32"], lambda: V.scalar_tensor_tensor(
                out=n32[:, 4 * h:4 * h + 4], in0=n32[:, 4 * h:4 * h + 4], scalar=decayAll[:, tt, h:h + 1], in1=ptS[:, 128:132], op0=ALU.mult,
                op1=ALU.add))
            op(DVE, ["n32"], [nbn[par]], lambda: V.tensor_copy(nb[par][:], n32[:, 4 * h:4 * h + 4]))

        def C1(tt):
            par = tt % 2
            cs_ = csm[par]
            cn = "csm%d" % par
            ptS, pbS = psb(2 + par)
            ptN, pbN = psb(par)
            op(DVE, [pbN], [cn], lambda: V.bn_stats(out=cs_[:, 0:6], in_=ptN[:, :]))
            op(DVE, [cn], [cn], lambda: V.bn_aggr(out=cs_[:, 6:8], in_=cs_[:, 0:6]))
            op(ACT, [pbS], [cn], lambda: Aeng.activation(out=cs_[:, 15:16], in_=ptS[:, 136:137], func=AF.Abs))
            op(DVE, [cn, "clampAll"], [cn], lambda: V.tensor_tensor(out=cs_[:, 15:16], in0=cs_[:, 15:16], in1=clampAll[:, tt, h:h + 1], op=ALU.max))
            op(DVE, [cn], [cn], lambda: V.reciprocal(out=cs_[:, 16:17], in_=cs_[:, 15:16]))
            op(DVE, [cn], [cn], lambda: V.scalar_tensor_tensor(out=cs_[:, 17:18], in0=cs_[:, 7:8], scalar=cs_[:, 16:17], in1=cs_[:, 16:17],
                                                              op0=ALU.mult, op1=ALU.mult))
            op(ACT, [cn], [cn], lambda: Aeng.activation(out=cs_[:, 8:9], in_=cs_[:, 17:18], func=AF.Ln, bias=EPS))
            op(ACT, [cn], [cn], lambda: Aeng.activation(out=cs_[:, 8:9], in_=cs_[:, 8:9], func=AF.Exp, scale=-0.5))

        def C2(tt):
            par = tt % 2
            cs_ = csm[par]
            cn = "csm%d" % par
            ptN, pbN = psb(par)
            hnn = "hn%d" % par
            op(DVE, [cn], [cn], lambda: V.tensor_tensor(out=cs_[:, 18:19], in0=cs_[:, 16:17], in1=cs_[:, 8:9], op=ALU.mult))
            op(DVE, [cn], [cn], lambda: V.scalar_tensor_tensor(out=cs_[:, 19:20], in0=cs_[:, 6:7], scalar=-1.0, in1=cs_[:, 18:19],
                                                              op0=ALU.mult, op1=ALU.mult))
            op(ACT, [pbN, cn], [hnn], lambda: Aeng.activation(out=hn[par][:], in_=ptN[:, :], func=AF.Identity, scale=cs_[:, 18:19],
                                                             bias=cs_[:, 19:20]))
            ptT, pbT = psb(7)
            ptTb = ptT[:].bitcast(BF16).rearrange("p (c t) -> p c t", c=8)
            for c in range(4):
                op(PE, [hnn, "ident"], [pbT], lambda c=c: Pm.transpose(ptTb[:, c, :], hn[par][:, c * 128:(c + 1) * 128], ident[:]))

        def C3(tt):
            tk = slice(tt * 128, (tt + 1) * 128)
            ptT, pbT = psb(7)
            ptTb = ptT[:].bitcast(BF16).rearrange("p (c t) -> p c t", c=8)
            par = tt % 2
            tmpg = qs[par]
            op(DVE, [pbT, "szh"], ["qs%d" % par], lambda: V.tensor_tensor(out=tmpg[:], in0=ptTb[:, 0:4, :], in1=szh[:, :, tk], op=ALU.mult))
            op(DVE, ["qs%d" % par, "A2"], ["A2"], lambda: V.tensor_tensor(out=A2[:, 4 * h:4 * h + 4, tk], in0=tmpg[:],
                                                                       in1=A2[:, 4 * h:4 * h + 4, tk], op=ALU.add))

        for tt in range(ntiles + 2):
            if tt < ntiles:
                F(tt)
                U(tt)
                N(tt)
            if 0 <= tt - 2 < ntiles:
                C3(tt - 2)
            if 0 <= tt - 1 < ntiles:
                C2(tt - 1)
            if tt < ntiles:
                C1(tt)

    def layer1(ntiles, nt, is_s, stride, before_outproj=None):
        N = ntiles * nt
        dg31 = g["dg31"]
        norm_mod(1, ntiles, nt, is_s)
        newc = slice(30 * stride, 30 * stride + N)
        for p in range(4):
            wa, wab = wget(g["cf_w_in"][:, p * 512:(p + 1) * 512], 8, 512, key=("cfa", p))
            abanks, agot = reserve(4)
            for fc in range(4):
                pa, pab = abanks[fc]

                def mma(pa=pa, fc=fc, wa=wa):
                    ins = None
                    for kc in range(8):
                        ins = Pm.matmul(pa[:, 0:N], lhsT=wa[:, kc, fc * 128:(fc + 1) * 128], rhs=hT[:, kc, 0:N], start=(kc == 0), stop=(kc == 7))
                    return ins
                op(PE, [wab, "hT"], [pab], mma)
            wgt, wgb = wget(g["cf_w_in"][:, E + p * 512:E + (p + 1) * 512], 8, 512, key=("cfg", p))
            for fc in range(4):
                ch = 4 * p + fc
                pa, pab = abanks[fc]
                pg, pgb = bank()

                def mmg(pg=pg, fc=fc, wgt=wgt):
                    ins = None
                    for kc in range(8):
                        ins = Pm.matmul(pg[:, 0:N], lhsT=wgt[:, kc, fc * 128:(fc + 1) * 128], rhs=hT[:, kc, 0:N], start=(kc == 0), stop=(kc == 7))
                    return ins
                op(PE, [wgb, "hT"], [pgb], mmg)
                a = ch % 2
                op(ACT, [pgb, "pcol"], ["hn%d" % a], lambda pg=pg, a=a, ch=ch: Aeng.activation(
                    out=sig[a][:, 0:N], in_=pg[:, 0:N], func=AF.Sigmoid, bias=pc("cf_b_in", 16 + ch)))
                udst = A2[:, ch, 496:512] if is_s else A1[:, ch, newc]
                op(DVE, [pab, "pcol", "hn%d" % a], ["A2h" if is_s else "A1"], lambda pa=pa, a=a, ch=ch, udst=udst: V.scalar_tensor_tensor(
                    out=udst, in0=pa[:, 0:N], scalar=pc("cf_b_in", ch), in1=sig[a][:, 0:N], op0=ALU.add, op1=ALU.mult))
            release(agot)
        handoff(HL, ["yT"])
        sbanks, sgot = reserve(2)
        (pmean, pmeanb), (pex2, pex2b) = sbanks
        GRP = [(0, 11, DVE, V), (11, 10, POOL, G), (21, 10, DVE, V)]
        for c in range(16):
            if is_s:
                tmpc = hh[0][:, 0:496]
                yv = hh[1][:, 0:NS]
                op(DVE, ["A2h", "pcol"], ["hh0"], lambda c=c: V.tensor_tensor(
                    out=tmpc.rearrange("p (j s) -> p j s", j=31), in0=A2[:, c, 16:512].rearrange("p (j s) -> p j s", j=31),
                    in1=pc("cf_w_dwT", 31 * c, 31).unsqueeze(2).to_broadcast([128, 31, NS]), op=ALU.mult))
                op(DVE, ["hh0"], ["hh1"], lambda: V.tensor_reduce(out=yv, in_=tmpc.rearrange("p (j s) -> p s j", j=31), axis=AX.X, op=ALU.add))
                op(ACT, ["hh1", "pcol"], ["yT"], lambda c=c: Aeng.activation(out=yT[:, c, 0:N], in_=yv, func=AF.Identity, bias=pc("cf_b_dw", c)))
                a = c % 2
                op(ACT, ["hh1", "pcol"], ["kw%d" % a], lambda c=c, a=a: Aeng.activation(out=ysq[a][:, 0:N], in_=yv, func=AF.Square,
                                                                                     bias=pc("cf_b_dw", c)))
                op(PE, ["yT", "onesE"], [pmeanb], lambda c=c: Pm.matmul(pmean[:, 0:N], lhsT=onesE[:], rhs=yT[:, c, 0:N], start=(c == 0),
                                                                       stop=(c == 15)))
                op(PE, ["kw%d" % a, "onesE"], [pex2b], lambda c=c, a=a: Pm.matmul(pex2[:, 0:N], lhsT=onesE[:], rhs=ysq[a][:, 0:N],
                                                                                start=(c == 0), stop=(c == 15)))
                continue
            pt, pb = bank()
            for gi, (j0, nj, Eg_, En_) in enumerate(GRP):
                op(Eg_, ["ident", "pcol"], ["dg31_%d" % gi], lambda c=c, gi=gi, j0=j0, nj=nj, En_=En_: En_.tensor_tensor(
                    out=dg31[gi][:, 0:nj, :], in0=ident[:].unsqueeze(1).to_broadcast([128, nj, 128]),
                    in1=pc("cf_w_dwT", 31 * c + j0, nj).unsqueeze(2).to_broadcast([128, nj, 128]), op=ALU.mult))

                def mm(c=c, pt=pt, gi=gi, j0=j0, nj=nj):
                    ins = None
                    for jj in range(nj):
                        j = j0 + jj
                        ins = Pm.matmul(pt[:, 0:N], lhsT=dg31[gi][:, jj, :], rhs=A1[:, c, j * stride:j * stride + N], start=(j == 0),
                                        stop=(j == 30))
                    return ins
                op(PE, ["dg31_%d" % gi, "A1"], [pb], mm)
            op(ACT, [pb, "pcol"], ["yT"], lambda c=c, pt=pt: Aeng.activation(out=yT[:, c, 0:N], in_=pt[:, 0:N], func=AF.Identity,
                                                                           bias=pc("cf_b_dw", c)))
            a = c % 2
            op(ACT, [pb, "pcol"], ["kw%d" % a], lambda c=c, pt=pt, a=a: Aeng.activation(out=ysq[a][:, 0:N], in_=pt[:, 0:N], func=AF.Square,
                                                                                   bias=pc("cf_b_dw", c)))
            op(PE, ["yT", "onesE"], [pmeanb], lambda c=c: Pm.matmul(pmean[:, 0:N], lhsT=onesE[:], rhs=yT[:, c, 0:N], start=(c == 0),
                                                                   stop=(c == 15)))
            op(PE, ["kw%d" % a, "onesE"], [pex2b], lambda c=c, a=a: Pm.matmul(pex2[:, 0:N], lhsT=onesE[:], rhs=ysq[a][:, 0:N],
                                                                            start=(c == 0), stop=(c == 15)))
        op(DVE, [pmeanb], ["rn"], lambda: V.tensor_copy(nmr_bc[:, 0:N], pmean[:, 0:N]))
        op(DVE, ["rn"], ["rn"], lambda: V.tensor_tensor(out=rstd_bc[:, 0:N], in0=nmr_bc[:, 0:N], in1=nmr_bc[:, 0:N], op=ALU.mult))
        op(DVE, [pex2b, "rn"], ["rn"], lambda: V.tensor_tensor(out=rstd_bc[:, 0:N], in0=pex2[:, 0:N], in1=rstd_bc[:, 0:N], op=ALU.subtract))
        release(sgot)
        op(ACT, ["rn"], ["rn"], lambda: Aeng.activation(out=rstd_bc[:, 0:N], in_=rstd_bc[:, 0:N], func=AF.Ln, bias=EPS))
        op(ACT, ["rn"], ["rn"], lambda: Aeng.activation(out=rstd_bc[:, 0:N], in_=rstd_bc[:, 0:N], func=AF.Exp, scale=-0.5))
        op(DVE, ["rn"], ["rn"], lambda: V.scalar_tensor_tensor(out=nmr_bc[:, 0:N], in0=nmr_bc[:, 0:N], scalar=-1.0, in1=rstd_bc[:, 0:N],
                                                              op0=ALU.mult, op1=ALU.mult))
        for p in range(4):
            wv, wb = wget(g["cf_w_in"][:, 2 * E + p * 512:2 * E + (p + 1) * 512], 8, 512, key=("cfz", p))

            def ev(fc, pt, pb, p=p):
                ch = 4 * p + fc
                a = ch % 2
                op(ACT, [pb, "pcol"], ["hn%d" % a], lambda: Aeng.activation(out=sig[a][:, 0:N], in_=pt[:, 0:N], func=AF.Silu,
                                                                          bias=pc("cf_b_in", 32 + ch)))
                op(DVE, ["yT", "rn"], ["hh%d" % a], lambda: V.tensor_tensor(out=lt1[a][:, 0:N], in0=yT[:, ch, 0:N], in1=rstd_bc[:, 0:N],
                                                                        op=ALU.mult))
                op(DVE, ["hh%d" % a, "rn"], ["hh%d" % a], lambda: V.tensor_tensor(out=lt1[a][:, 0:N], in0=lt1[a][:, 0:N], in1=nmr_bc[:, 0:N],
                                                                              op=ALU.add))
                op(ACT, ["hh%d" % a, "pcol"], ["qs%d" % a], lambda: Aeng.activation(out=lt2[a][:, 0:N], in_=lt1[a][:, 0:N], func=AF.Silu,
                                                                                  scale=pc("cf_ln_g", ch), bias=pc("cf_ln_b", ch)))
                op(DVE, ["qs%d" % a, "hn%d" % a], ["A2"], lambda: V.tensor_tensor(out=A2[:, ch, 0:N], in0=lt2[a][:, 0:N], in1=sig[a][:, 0:N],
                                                                              op=ALU.mult))
            fm_group(wv, wb, lambda kc: hT[:, kc, 0:N], ["hT"], 8, N, 4, ev)
        if before_outproj is not None:
            before_outproj()
        outproj_resid(1, g["cf_w_out"], ntiles, nt)

    def final_out(ntiles, nt, dst_rows):
        for tt in range(ntiles):
            rs, sn_ = rms_rows(xres[0:nt, tt, :], nt)
            for hf in range(2):
                op(DVE, ["xres", sn_, "fg_bc"], ["hh%d" % hf], lambda hf=hf, tt=tt, rs=rs: V.scalar_tensor_tensor(
                    out=hh[hf][0:nt, :], in0=xres[0:nt, tt, hf * 512:(hf + 1) * 512], scalar=rs,
                    in1=fg_bc[0:nt, hf * 512:(hf + 1) * 512], op0=ALU.mult, op1=ALU.mult))
                dma(SP, ["hh%d" % hf], [], lambda hf=hf, tt=tt: nc.sync.dma_start(
                    out=dst_rows(tt)[:, hf * 512:(hf + 1) * 512], in_=hh[hf][0:nt, :]))

    ssm, ntok, hs, dexp, dec_bc, kwm = (g[k] for k in ["ssm", "ntok", "hs", "dexp", "dec_bc", "kwm"])
    Cs_, nso, mso, mcs, ccs = g["Cs"], g["nso"], g["mso"], g["mcs"], g["ccs"]
    cslot = [C32[:, 4 * i:4 * i + 4, :] for i in range(4)]
    stg = C32[:, 0:4, :].rearrange("p a b -> p (a b)")
    qdiag = A1[:, :, 64:320].rearrange("p c (s m) -> p c s m", s=16)
    cbs = [yT[:, 4 * i:4 * i + 4, :] for i in range(4)]
    szs = qs[0][:].rearrange("p c t -> p (c t)")

    dma(SP, [], ["xres"], lambda: nc.sync.dma_start(out=xres[0:NS, 0, :], in_=xs))
    for j in range(3):
        dma(SP, [], ["stg"], lambda j=j: nc.sync.dma_start(out=stg[j * NS:(j + 1) * NS, :], in_=smc[:, j, :]))
    for half in range(2):
        pt, pb = bank()
        for cc_ in range(8):
            c = half * 8 + cc_
            op(PE, ["stg", "identf"], [pb], lambda c=c, cc_=cc_, pt=pt: Pm.transpose(pt[:, cc_ * 48:(cc_ + 1) * 48], stg[0:48, c * 128:(c + 1) * 128],
                                                                                 identf[0:48, 0:48]))
        op(DVE, [pb], ["A1"], lambda half=half, pt=pt: V.tensor_copy(A1[:, half * 8:(half + 1) * 8, 0:48],
                                                                      pt[:, 0:384].rearrange("p (c r) -> p c r", c=8)))
    layer0_front(1, NS, True, NS)
    stage(2)
    dma(SP, [], [], lambda: nc.sync.dma_start(out=mcs[:, 0:2, :], in_=smc[:, 1:3, :]))
    rows_out(lambda c: A1[:, c, 48:64], ["A1"], NS, lambda q: mcs[:, 2, q * 512:(q + 1) * 512])
    dma(SP, [], ["ssm"], lambda: nc.sync.dma_start(out=ssm[:, 0:4], in_=sm))
    igs, lfs = igt[0:NS, 0, :], lft[0:NS, 0, :]
    c_ = lambda a: ssm[:, 4 * a:4 * a + 4]
    op(DVE, ["ssm", "lft"], ["ssm"], lambda: V.tensor_tensor(out=c_(1), in0=c_(0), in1=lfs, op=ALU.add))
    op(DVE, ["ssm", "igt"], ["ssm"], lambda: V.tensor_tensor(out=c_(2), in0=c_(1), in1=igs, op=ALU.max))
    op(DVE, ["ssm", "igt"], ["ssm"], lambda: V.tensor_tensor(out=c_(3), in0=igs, in1=c_(2), op=ALU.subtract))
    op(ACT, ["ssm"], ["ssm"], lambda: Aeng.activation(out=c_(3), in_=c_(3), func=AF.Exp))
    op(DVE, ["ssm"], ["ssm"], lambda: V.tensor_tensor(out=c_(4), in0=c_(1), in1=c_(2), op=ALU.subtract))
    op(ACT, ["ssm"], ["ssm"], lambda: Aeng.activation(out=c_(4), in_=c_(4), func=AF.Exp))
    op(ACT, ["ssm"], ["ssm"], lambda: Aeng.activation(out=c_(5), in_=c_(2), func=AF.Exp, scale=-1.0))
    dma(SP, ["ssm"], [], lambda: nc.sync.dma_start(out=mso, in_=c_(2)))
    handoff(["stg"], ["cslot0", "cslot1", "cslot2", "cslot3"])
    for h in range(H):
        hsl = slice(h * 512, (h + 1) * 512)
        head_proj(h, 1, NS, NS, NS, True)
        for c in range(4):
            op(DVE, ["qTh", "eye16"], ["qdiag"], lambda c=c, h=h: V.tensor_tensor(
                out=qdiag[:, 4 * h + c, :, :], in0=qTh[:, c, 0:NS].unsqueeze(2).to_broadcast([128, NS, NS]), in1=eye16[:], op=ALU.mult))
        scale_xc_skip(h, NS)
        op(DVE, ["szh"], ["qs0"], lambda h=h: V.tensor_copy(szs[:, h * 64:h * 64 + 64].rearrange("p (c s) -> p c s", c=4), szh[:, :, 0:NS]))
        dma(SP, [], ["ntok"], lambda hsl=hsl: nc.sync.dma_start(out=ntok[:], in_=sn[:, hsl]))
        op(DVE, ["qtok", "ktok"], ["hs"], lambda h=h: V.tensor_tensor(out=hs[:], in0=qtok[:], in1=ktok[0:NS, h, :], op=ALU.mult))
        op(DVE, ["hs"], ["ssm"], lambda h=h: V.tensor_reduce(out=ssm[:, 24 + h:25 + h], in_=hs[:], axis=AX.X, op=ALU.add))
        op(DVE, ["qtok", "ntok"], ["hs"], lambda: V.tensor_tensor(out=hs[:], in0=qtok[:], in1=ntok[:], op=ALU.mult))
        op(DVE, ["hs"], ["ssm"], lambda h=h: V.tensor_reduce(out=ssm[:, 28 + h:29 + h], in_=hs[:], axis=AX.X, op=ALU.add))
        op(DVE, ["ktok", "ssm"], ["ktok"], lambda h=h: V.tensor_scalar(out=ktok[0:NS, h, :], in0=ktok[0:NS, h, :], scalar1=ssm[:, 12 + h:13 + h],
                                                                     scalar2=None, op0=ALU.mult))
        op(DVE, ["ntok", "ssm", "ktok"], ["ntok"], lambda h=h: V.scalar_tensor_tensor(
            out=ntok[:], in0=ntok[:], scalar=ssm[:, 16 + h:17 + h], in1=ktok[0:NS, h, :], op0=ALU.mult, op1=ALU.add))
        dma(SP, ["ntok"], [], lambda hsl=hsl: nc.sync.dma_start(out=nso[:, hsl], in_=ntok[:]))
    op(DVE, ["ssm"], ["ssm"], lambda: V.tensor_tensor(out=c_(8), in0=c_(6), in1=c_(3), op=ALU.mult))
    op(DVE, ["ssm"], ["ssm"], lambda: V.tensor_tensor(out=c_(9), in0=c_(7), in1=c_(4), op=ALU.mult))
    op(DVE, ["ssm"], ["ssm"], lambda: V.tensor_tensor(out=c_(9), in0=c_(9), in1=c_(8), op=ALU.add))
    op(ACT, ["ssm"], ["ssm"], lambda: Aeng.activation(out=c_(9), in_=c_(9), func=AF.Abs))
    op(DVE, ["ssm"], ["ssm"], lambda: V.tensor_tensor(out=c_(9), in0=c_(9), in1=c_(5), op=ALU.max))
    op(DVE, ["ssm"], ["ssm"], lambda: V.reciprocal(out=c_(10), in_=c_(9)))
    op(DVE, ["ssm", "identf"], ["dexp"], lambda: V.tensor_tensor(
        out=dexp[:].rearrange("p (s h) -> p s h", s=NS), in0=identf[0:NS, 0:NS].unsqueeze(2).to_broadcast([NS, NS, 4]),
        in1=c_(4).unsqueeze(1).to_broadcast([NS, NS, 4]), op=ALU.mult))
    pt, pb = bank()
    op(PE, ["dexp", "onesf"], [pb], lambda pt=pt: Pm.matmul(pt[:, 0:64], lhsT=onesf[0:NS, :], rhs=dexp[:], start=True, stop=True))
    op(DVE, [pb], ["dec_bc"], lambda pt=pt: V.tensor_copy(dec_bc[:], pt[:, 0:64]))
    stage(3)
    stg2 = xres[:, 1:3, :].rearrange("p a b -> p (a b)")
    for i in range(4):
        nj = 8 if i < 3 else 6
        rws = nj * NS
        for jj in range(nj):
            dma(SP, [], ["xres"], lambda i=i, jj=jj: nc.sync.dma_start(out=stg2[jj * NS:(jj + 1) * NS, :], in_=scc[:, 8 * i + jj, :]))
        for q4 in range(4):
            pt, pb = bank()
            for cc_ in range(4):
                c = q4 * 4 + cc_
                op(PE, ["xres", "identf"], [pb], lambda c=c, cc_=cc_, pt=pt, rws=rws: Pm.transpose(
                    pt[:, cc_ * 128:cc_ * 128 + rws], stg2[0:rws, c * 128:(c + 1) * 128], identf[0:rws, 0:rws]))
            op(DVE, [pb], ["A2h"], lambda q4=q4, pt=pt, i=i, rws=rws: V.tensor_copy(
                A2[:, q4 * 4:q4 * 4 + 4, 16 + 128 * i:16 + 128 * i + rws], pt[:, :].rearrange("p (c r) -> p c r", c=4)[:, :, 0:rws]))
    handoff(HL, ["cbs%d" % i for i in range(4)])
    qcb, qgot = reserve(4)
    tiles = [(s_, h_) for s_ in range(NS) for h_ in range(H)]

    def c_load(k):
        s_, h_ = tiles[k]
        cs_t = cslot[k % 4]
        dma(SP, [], ["cslot%d" % (k % 4)], lambda: nc.sync.dma_start(out=cs_t, in_=sC[s_, h_].rearrange("(c p) v -> p c v", p=128)))
    c_load(0)
    c_load(1)
    c_load(2)
    for k_i, (s, h) in enumerate(tiles):
        if k_i + 3 < len(tiles):
            c_load(k_i + 3)
        sl = k_i % 4
        cb = k_i % 4
        cs_t = cslot[sl]
        op(ACT, ["cslot%d" % sl], ["cbs%d" % cb], lambda cs_t=cs_t, cb=cb: Aeng.activation(out=cbs[cb], in_=cs_t, func=AF.Copy))
        qp, qpb = qcb[h]

        def mmq(s=s, h=h, cb=cb, qp=qp):
            ins = None
            for c in range(4):
                ins = Pm.matmul(qp[0:NS, :], lhsT=qdiag[:, 4 * h + c, s, :], rhs=cbs[cb][:, c, :], start=(s == 0 and c == 0),
                                stop=(s == NS - 1 and c == 3))
            return ins
        op(PE, ["qdiag", "cbs%d" % cb], [qpb], mmq)
        km = kwm[k_i % 2]
        kn = "kwm%d" % (k_i % 2)
        op(DVE, ["ktok", "identf"], [kn], lambda s=s, h=h, km=km: V.tensor_scalar(
            out=km[:], in0=ktok[0:NS, h, :], scalar1=identf[0:NS, s:s + 1], scalar2=None, op0=ALU.mult))
        for c in range(4):
            ptC, pbC = bank()
            op(PE, [kn, "vtok"], [pbC], lambda c=c, ptC=ptC, km=km, h=h: Pm.matmul(
                ptC[:, :], lhsT=km[:, c * 128:(c + 1) * 128], rhs=vtok[0:NS, h, :], start=True, stop=True))
            op(DVE, [pbC, "dec_bc", "cslot%d" % sl], ["cslot%d" % sl], lambda c=c, ptC=ptC, cs_t=cs_t, s=s, h=h: V.scalar_tensor_tensor(
                out=cs_t[:, c, :], in0=cs_t[:, c, :], scalar=dec_bc[:, s * 4 + h:s * 4 + h + 1], in1=ptC[:, :], op0=ALU.mult, op1=ALU.add))
        dma(POOL, ["cslot%d" % sl], [], lambda s=s, h=h, cs_t=cs_t: G.dma_start(
            out=Cs_[s, h].rearrange("(c p) v -> p c v", p=128), in_=cs_t))
    stage(4)
    for h in range(H):
        qp, qpb = qcb[h]
        op(DVE, [qpb, "ssm"], ["hs"], lambda qp=qp, h=h: V.tensor_scalar(out=hs[:], in0=qp[0:NS, :], scalar1=ssm[:, 16 + h:17 + h],
                                                                     scalar2=None, op0=ALU.mult))
        op(DVE, ["vtok", "ssm", "hs"], ["hs"], lambda h=h: V.scalar_tensor_tensor(
            out=hs[:], in0=vtok[0:NS, h, :], scalar=ssm[:, 32 + h:33 + h], in1=hs[:], op0=ALU.mult, op1=ALU.add))
        op(DVE, ["hs", "ssm"], ["hs"], lambda h=h: V.tensor_scalar(out=hs[:], in0=hs[:], scalar1=ssm[:, 40 + h:41 + h], scalar2=None,
                                                               op0=ALU.mult))
        ln_rows(0, NS, hs[:], "hs", hn[0][0:NS, :])
        pt, pb = bank()
        ptb = pt[:].bitcast(BF16).rearrange("p (c t) -> p c t", c=8)
        for c in range(4):
            op(PE, ["hn0", "ident"], [pb], lambda c=c, ptb=ptb: Pm.transpose(ptb[:, c, 0:NS], hn[0][0:NS, c * 128:(c + 1) * 128],
                                                                          ident[0:NS, 0:NS]))
        for c in range(4):
            ch = 4 * h + c
            op(DVE, [pb, "pcol", "A2"], ["A2"], lambda c=c, ch=ch, ptb=ptb: V.scalar_tensor_tensor(
                out=A2[:, ch, 0:NS], in0=ptb[:, c, 0:NS], scalar=pc("ml_ln_g", ch), in1=A2[:, ch, 0:NS], op0=ALU.mult, op1=ALU.add))
            op(DVE, ["A2", "qs0"], ["A2"], lambda c=c, ch=ch, h=h: V.tensor_tensor(
                out=A2[:, ch, 0:NS], in0=A2[:, ch, 0:NS], in1=szs[:, h * 64 + c * 16:h * 64 + c * 16 + 16], op=ALU.mult))
    release(qgot)
    outproj_resid(0, g["ml_w_out"], 1, NS)
    stage(5)
    handoff(["cslot0", "cslot1", "cslot2", "cslot3"], ["stg"])
    handoff(["qdiag"], ["A1"])
    handoff(["cbs%d" % i for i in range(4)], HL + ["yT"])
    layer1(1, NS, True, NS)
    dma(SP, [], [], lambda: nc.sync.dma_start(out=ccs[:, 0:29, :], in_=scc[:, 1:30, :]))
    rows_out(lambda c: A2[:, c, 496:512], ["A2h"], NS, lambda q: ccs[:, 29, q * 512:(q + 1) * 512])
    final_out(1, NS, lambda tt: g["ys"])

    stage(6)
    for l in range(2):
        for hf in range(2):
            pt, pb = bank()
            op(PE, ["gate_t%d" % l, "sel16"], [pb], lambda pt=pt, hf=hf, l=l: Pm.matmul(
                pt[:, :], lhsT=sel16[:], rhs=gate_t[l][0:R, hf * 512:(hf + 1) * 512], start=True, stop=True))
            op(DVE, [pb], ["gate_t%d" % l], lambda pt=pt, hf=hf, l=l: V.tensor_copy(gate_t[l][:, hf * 512:(hf + 1) * 512], pt[:, :]))
    handoff(["stg", "cslot0", "cslot1", "cslot2", "cslot3"], ["C32_%d_%d" % (h, c) for h in range(H) for c in range(4)])
    handoff(["ntok", "hs", "qtok", "kwm0", "kwm1"], ["WtAll", "wintAll"])
    handoff(["A2h"], ["A2"])
    for h in range(H):
        op(POOL, [], ["C32_%d_%d" % (h, c) for c in range(4)], lambda h=h: G.memset(C32[:, 4 * h:4 * h + 4, :], 0.0))
    op(DVE, [], ["n32"], lambda: V.memset(n32[:], 0.0))
    op(DVE, [], ["mprev"], lambda: V.memset(mprev[:], 0.0))
    op(DVE, [], ["mhist"], lambda: V.memset(mhist[:], 0.0))
    op(DVE, [], ["chist"], lambda: V.memset(chist[:], 0.0))
    def early_norm(b):
        tb0 = b * TB
        norm_mod(0, 4, 128, False, stage_src=lambda tt: xp[tb0 + tt * 128:tb0 + (tt + 1) * 128, :])

    early_norm(0)
    for blk in range(NBLK):
        t0 = blk * TB
        for tt in range(4):
            dma(SP, [], ["xres"], lambda tt=tt: nc.sync.dma_start(out=xres[:, tt, :], in_=xp[t0 + tt * 128:t0 + (tt + 1) * 128, :]))
        op(DVE, ["mhist"], ["A1"], lambda: V.tensor_copy(A1[:, :, 0:3], mhist[:]))
        layer0_front(4, 128, False, 1, skip_norm=True)
        op(DVE, ["A1"], ["mhist"], lambda: V.tensor_copy(mhist[:], A1[:, :, TB:TB + 3]))
        if blk == NBLK - 1:
            rows_out(lambda c: A1[:, c, TB:TB + 3], ["A1"], 3, lambda q: g["mcp"][:, q * 512:(q + 1) * 512])
        if blk == 0:
            stage(7)
        if blk == 0:
            stage(71)
        for h in range(H):
            head_proj(h, 4, 128, TB, 1, False, between=(mchain if h == 0 else None))
            gate_prep(h, TB)
            if blk == 0 and h == 0:
                stage(72)
            prompt_cell(h, 4)
            if blk == 0 and h == 0:
                stage(73)
        if blk == 0:
            stage(8)
        outproj_resid(0, g["ml_w_out"], 4, 128)
        if blk == 0:
            stage(9)
        op(DVE, ["chist"], ["A1"], lambda: V.tensor_copy(A1[:, :, 0:30], chist[:]))
        layer1(4, 128, False, 1, before_outproj=((lambda blk=blk: early_norm(blk + 1)) if blk + 1 < NBLK else None))
        op(DVE, ["A1"], ["chist"], lambda: V.tensor_copy(chist[:], A1[:, :, TB:TB + 30]))
        if blk == NBLK - 1:
            rows_out(lambda c: A1[:, c, TB:TB + 30], ["A1"], 30, lambda q: g["ccp"][:, q * 512:(q + 1) * 512])
        final_out(4, 128, lambda tt: g["yp"][t0 + tt * 128:t0 + (tt + 1) * 128, :])
        if blk == 0:
            stage(10)
    for h in range(H):
        dma(SP, ["C32_%d_%d" % (h, c) for c in range(4)], [], lambda h=h: nc.sync.dma_start(out=g["Cp"][h].rearrange("(c p) v -> p c v", p=128),
                                                                 in_=C32[:, 4 * h:4 * h + 4, :]))
    dma(SP, ["n32"], [], lambda: nc.sync.dma_start(out=g["npo"].rearrange("h (c p) -> p h c", p=128),
                                                   in_=n32[:].rearrange("p (h c) -> p h c", h=4), allow_slow_non_contiguous=True))
    dma(SP, ["mprev"], [], lambda: nc.sync.dma_start(out=g["mpo"], in_=mprev[0:1, :]))
    T.finish()
    if not dry:
        print("emit: ops=%d waits=%d weight_panels=%d" % (T.nops, T.nwaits, len(ws["descs"])))


def _cols(v):
    v = np.ascontiguousarray(v, dtype=np.float32).reshape(-1)
    return v.reshape(-1, 128).T


_NC_CACHE = {}
_PREP_ONLY = False


def kernel(x_prompt, x_sample, c_prompt, c_sample, state_mlstm_C, state_mlstm_n, state_mlstm_m, state_mlstm_conv,
           state_conf_conv, norm_g, w_ada, b_ada, ml_w_in, ml_w_conv, ml_b_conv, ml_w_q, ml_w_k, ml_w_v, ml_w_ig, ml_b_ig,
           ml_w_fg, ml_b_fg, ml_ln_g, ml_skip, ml_w_out, cf_w_in, cf_b_in, cf_w_dw, cf_b_dw, cf_ln_g, cf_ln_b, cf_w_out, final_g):
    f = lambda a: np.ascontiguousarray(np.asarray(a), dtype=np.float32)
    pcol = np.zeros((128, NPCOL), np.float32)

    def put(name, arr):
        pcol[:, PCOL[name]:PCOL[name] + arr.shape[1]] = arr
    put("norm_g0", _cols(norm_g[0])); put("norm_g1", _cols(norm_g[1]))
    put("b_ada0", _cols(b_ada[0])); put("b_ada1", _cols(b_ada[1]))
    put("ml_b_conv", _cols(ml_b_conv[0])); put("ml_ln_g", _cols(ml_ln_g[0])); put("ml_skip", _cols(ml_skip[0]))
    put("cf_b_in", _cols(cf_b_in[0])); put("cf_b_dw", _cols(cf_b_dw[0])); put("cf_ln_g", _cols(cf_ln_g[0])); put("cf_ln_b", _cols(cf_ln_b[0]))
    wc = f(ml_w_conv[0]).T.reshape(16, 128, 4).transpose(1, 0, 2).reshape(128, 64)
    put("ml_w_convT", wc)
    wd = f(cf_w_dw[0]).T.reshape(16, 128, 31).transpose(1, 0, 2).reshape(128, 496)
    put("cf_w_dwT", wd)
    shared = dict(
        pcold=pcol, w_ada=f(w_ada), b_gate=f(np.asarray(b_ada)[:, 2 * D:3 * D]), ml_w_in=f(ml_w_in[0]),
        ml_w_q=f(ml_w_q[0]), ml_w_k=f(ml_w_k[0]), ml_w_v=f(ml_w_v[0]),
        w_gates=f(np.concatenate([np.asarray(ml_w_ig[0]), np.asarray(ml_w_fg[0])], axis=1)),
        b_gates=f(np.concatenate([np.asarray(ml_b_ig[0]), np.asarray(ml_b_fg[0])], axis=0)),
        ml_w_out=f(ml_w_out[0]), cf_w_in=f(cf_w_in[0]), cf_w_out=f(cf_w_out[0]), final_g=f(final_g))
    xpn, xsn, cpn, csn = f(x_prompt), f(x_sample), f(c_prompt), f(c_sample)
    sCn, snn, smn, smcn, sccn = f(state_mlstm_C), f(state_mlstm_n), f(state_mlstm_m), f(state_mlstm_conv), f(state_conf_conv)
    in_maps = []
    for i in range(NCORES):
        s0, s1 = i * NS, (i + 1) * NS
        m = dict(shared)
        m.update(xp=xpn[i], xs=xsn[s0:s1, 0, :], cc=np.concatenate([csn[s0:s1], cpn[i:i + 1]], axis=0),
                 sC=sCn[0, s0:s1], sn=snn[0, s0:s1].reshape(NS, E), sm=smn[0, s0:s1], smc=smcn[0, s0:s1], scc=sccn[0, s0:s1])
        in_maps.append(m)
    if _PREP_ONLY:
        return in_maps
    if "nc" not in _NC_CACHE:
        _NC_CACHE["nc"] = build_nc()
    res = run_bass_kernel_spmd(_NC_CACHE["nc"], in_maps, core_ids=list(range(NCORES)))
    r = res.results
    st = lambda k: np.stack([np.asarray(r[i][k], dtype=np.float32) for i in range(NCORES)], axis=0)
    cat = lambda k: np.concatenate([np.asarray(r[i][k], dtype=np.float32) for i in range(NCORES)], axis=0)
    y_prompt = st("yp")
    y_sample = cat("ys").reshape(NCORES * NS, 1, D)
    C_p = st("Cp")[None]
    C_s = cat("Cs")[None]
    n_p = st("npo")[None]
    n_s = cat("nso").reshape(NCORES * NS, H, DH)[None]
    m_p = st("mpo").reshape(NCORES, H)[None]
    m_s = cat("mso")[None]
    mc_p = st("mcp")[None]
    mc_s = cat("mcs")[None]
    cc_p = st("ccp")[None]
    cc_s = cat("ccs")[None]
    return (y_prompt, y_sample, C_p, C_s, n_p, n_s, m_p, m_s, mc_p, mc_s, cc_p, cc_s)
```

```python
import numpy as np
import concourse.bass as bass
import concourse.mybir as mybir
from concourse.bass_utils import run_bass_kernel_spmd

F32 = mybir.dt.float32
BF16 = mybir.dt.bfloat16
ALU = mybir.AluOpType
AF = mybir.ActivationFunctionType
AX = mybir.AxisListType

D = 1024
E = 2048
H = 4
DH = 512
SEQ = 2048
NS = 16
TB = 512
NBLK = SEQ // TB
EPS = 1e-6
BIG = 30000.0
NCORES = 8
NWC = 40
WCACHE = False


class Buf:
    __slots__ = ("name", "w", "r")

    def __init__(self, name):
        self.name = name
        self.w = None
        self.r = {}


class Eng:
    def __init__(self, trk, eng, name, is_pe=False):
        self.trk = trk
        self.eng = eng
        self.name = name
        self.key = "E" + name
        self.cnt = 0
        self.seen = {}
        self.is_pe = is_pe
        if not trk.dry:
            trk.sems[self.key] = trk.nc.alloc_semaphore("s_" + name)

    def wait(self, ev):
        if ev is None:
            return
        key, val = ev
        if self.is_pe and key == self.key:
            return
        if self.seen.get(key, 0) >= val:
            return
        self.seen[key] = val
        if not self.trk.dry:
            self.eng.wait_ge(self.trk.sems[key], val)
        self.trk.nwaits += 1


class Trk:
    def __init__(self, nc, dry, n_dma_sems=32):
        self.nc = nc
        self.dry = dry
        self.sems = {}
        self.nwaits = 0
        self.nops = 0
        self.pe = Eng(self, nc.tensor, "pe", is_pe=True)
        self.dve = Eng(self, nc.vector, "dve")
        self.act = Eng(self, nc.scalar, "act")
        self.pool = Eng(self, nc.gpsimd, "pool")
        self.sp = Eng(self, nc.sync, "sp")
        self.dsem = {"hw": [], "sw": []}
        for kind, n in (("hw", 20), ("sw", 8)):
            for i in range(n):
                key = "D%s%d" % (kind, i)
                if not dry:
                    self.sems[key] = nc.alloc_semaphore("d_%s%d" % (kind, i))
                self.dsem[kind].append([key, 0])
        self.dnext = {"hw": 0, "sw": 0}

    def _deps(self, Eg, reads, writes):
        for b in reads:
            Eg.wait(b.w)
        for b in writes:
            Eg.wait(b.w)
            for k, v in b.r.items():
                Eg.wait((k, v))

    def _mark(self, ev, reads, writes):
        k, v = ev
        for b in reads:
            if b.r.get(k, 0) < v:
                b.r[k] = v
        for b in writes:
            b.w = ev
            b.r = {}

    def op(self, Eg, reads, writes, fn):
        ex = [b for b in reads if b.name.startswith("ps")]
        if ex:
            reads = [b for b in reads if not b.name.startswith("ps")]
            writes = list(writes) + ex
        self._deps(Eg, reads, writes)
        Eg.cnt += 1
        if not self.dry:
            ins = fn()
            ins.then_inc(self.sems[Eg.key], 1)
        self._mark((Eg.key, Eg.cnt), reads, writes)
        self.nops += 1

    def dma(self, Q, reads, writes, fn):
        self._deps(Q, reads, writes)
        kind = "sw" if Q is self.pool else "hw"
        pool = self.dsem[kind]
        slot = pool[self.dnext[kind]]
        self.dnext[kind] = (self.dnext[kind] + 1) % len(pool)
        if slot[1] > 0:
            Q.wait((slot[0], slot[1]))
        slot[1] += 16
        if not self.dry:
            ins = fn()
            ins.then_inc(self.sems[slot[0]], 16)
        self._mark((slot[0], slot[1]), reads, writes)

    def handoff(self, olds, news):
        for nb in news:
            for ob in olds:
                if ob.w is not None:
                    k, v = ob.w
                    if nb.r.get(k, 0) < v:
                        nb.r[k] = v
                for k, v in ob.r.items():
                    if nb.r.get(k, 0) < v:
                        nb.r[k] = v

    def finish(self):
        for kind in ("hw", "sw"):
            for slot in self.dsem[kind]:
                if slot[1] > 0:
                    self.sp.wait((slot[0], slot[1]))


PCOL = {}
_o = 0
for _n, _w in [("norm_g0", 8), ("norm_g1", 8), ("b_ada0", 24), ("b_ada1", 24), ("ml_b_conv", 16), ("ml_ln_g", 16),
               ("ml_skip", 16), ("cf_b_in", 48), ("cf_b_dw", 16), ("cf_ln_g", 16), ("cf_ln_b", 16),
               ("ml_w_convT", 64), ("cf_w_dwT", 496)]:
    PCOL[_n] = _o
    _o += _w
NPCOL = _o


def build_nc():
    nc = bass.Bass("TRN2", target_bir_lowering=False)

    def din(name, shape):
        return nc.dram_tensor(name, list(shape), F32, kind="ExternalInput").ap()

    def dout(name, shape):
        return nc.dram_tensor(name, list(shape), F32, kind="ExternalOutput").ap()

    xp = din("xp", [SEQ, D]); xs = din("xs", [NS, D]); cc = din("cc", [NS + 1, D])
    sC = din("sC", [NS, H, DH, DH]); sn = din("sn", [NS, E]); sm = din("sm", [NS, H])
    smc = din("smc", [NS, 3, E]); scc = din("scc", [NS, 30, E])
    pcol_d = din("pcold", [128, NPCOL])
    w_ada = din("w_ada", [2, D, 3 * D]); b_gate = din("b_gate", [2, D])
    ml_w_in = din("ml_w_in", [D, 2 * E])
    ml_w_q = din("ml_w_q", [H, DH, DH]); ml_w_k = din("ml_w_k", [H, DH, DH]); ml_w_v = din("ml_w_v", [H, DH, DH])
    w_gates = din("w_gates", [3 * E, 8]); b_gates = din("b_gates", [8])
    ml_w_out = din("ml_w_out", [E, D])
    cf_w_in = din("cf_w_in", [D, 3 * E]); cf_w_out = din("cf_w_out", [E, D])
    final_g = din("final_g", [D])

    wcache = nc.dram_tensor("wcache", [NWC, 128, 4096], BF16, kind="Internal").ap()
    yp = dout("yp", [SEQ, D]); ys = dout("ys", [NS, D])
    Cp = dout("Cp", [H, DH, DH]); Cs = dout("Cs", [NS, H, DH, DH])
    npo = dout("npo", [H, DH]); nso = dout("nso", [NS, E])
    mpo = dout("mpo", [1, H]); mso = dout("mso", [NS, H])
    mcp = dout("mcp", [3, E]); mcs = dout("mcs", [NS, 3, E])
    ccp = dout("ccp", [30, E]); ccs = dout("ccs", [NS, 30, E])

    def sb(name, shape, dt=F32):
        return nc.alloc_sbuf_tensor(name, list(shape), dt)

    ident = sb("ident", [128, 128], BF16); identf = sb("identf", [128, 128])
    onesf = sb("onesf", [128, 128]); tri = sb("tri", [128, 128])
    maskb = sb("maskb", [128, 128]); maskT = sb("maskT", [128, 128])
    onesE = sb("onesE", [128, 128], BF16); onesb = sb("onesb", [128, 1], BF16)
    eye16 = sb("eye16", [128, 16, 16], BF16); sel16 = sb("sel16", [NS + 1, 128])
    pcol = sb("pcol", [128, NPCOL])
    fg_bc = sb("fg_bc", [128, D]); bgt_bc = sb("bgt_bc", [128, 8])
    wg = sb("wg", [128, 48, 8], BF16)
    siluT = sb("siluT", [128, 8, NS + 1], BF16)
    modT = [sb("modT%d" % l, [128, 24, NS + 1]) for l in range(2)]
    gsc = [sb("gsc%d" % l, [128, 8, NS + 1]) for l in range(2)]
    gate_t = [sb("gate_t%d" % l, [128, D]) for l in range(2)]
    xres = sb("xres", [128, 4, D])
    xn = sb("xn", [128, D], BF16)
    sm1 = sb("sm1", [128, 64])
    hT = sb("hT", [128, 8, TB], BF16)
    A1 = sb("A1", [128, 16, TB + 30], BF16)
    A2 = sb("A2", [128, 16, TB], BF16)
    yT = sb("yT", [128, 16, TB], BF16)
    szh = yT[:, 0:4, :]; qTh = yT[:, 4:8, :]; kTh = yT[:, 8:12, :]; vTh = yT[:, 12:16, :]
    mhist = sb("mhist", [128, 16, 3], BF16); chist = sb("chist", [128, 16, 30], BF16)
    ktok = sb("ktok", [128, 4, DH], BF16); vtok = sb("vtok", [128, 4, DH], BF16)
    C32 = sb("C32", [128, 16, DH])
    rn = sb("rn", [128, 2 * TB])
    Cbf = rn[:].bitcast(BF16).rearrange("p (c v) -> p c v", c=4)
    rstd_bc = rn[:, 0:TB]; nmr_bc = rn[:, TB:2 * TB]
    n32 = sb("n32", [128, 16]); nbf = sb("nbf", [128, 4], BF16); nbf2 = sb("nbf2", [128, 4], BF16)
    mprev = sb("mprev", [128, 4])
    igt = sb("igt", [128, 4, 4]); lft = sb("lft", [128, 4, 4]); gpre = sb("gpre", [128, 8])
    bcol = sb("bcol", [128, 4, 4]); btot = sb("btot", [128, 4, 4]); gcol = sb("gcol", [128, 4, 4])
    wsl = [sb("wsl%d" % i, [128, 4096], BF16) for i in range(3)]
    dg4 = [sb("dg4_%d" % i, [128, 4, 128], BF16) for i in range(2)]
    dg31 = [sb("dg31_%d" % i, [128, 11, 128], BF16) for i in range(3)]
    St = [sb("St%d" % i, [128, 128], BF16) for i in range(2)]
    ovl = sb("ovl", [128, 4096], BF16)
    WtAll = ovl[:, 0:2048].rearrange("p (a b c) -> p a b c", a=4, b=4)
    wintAll = ovl[:, 2048:4096].rearrange("p (a b c) -> p a b c", a=4, b=4)
    xn2 = sb("xn2", [128, D], BF16)
    Gf = sb("Gf", [128, 2, 16, 8]); Gbf = sb("Gbf", [128, 2, 16, 8], BF16)
    xst = sb("xst", [128, D])
    clampAll = sb("clampAll", [128, 4, 4]); wkAll = sb("wkAll", [128, 4, 4]); decayAll = sb("decayAll", [128, 4, 4])
    mtmp = sb("mtmp", [128, 32])
    qs = [sb("qs%d" % i, [128, 4, 128], BF16) for i in range(2)]
    hh = [sb("hh%d" % i, [128, DH]) for i in range(2)]
    hn = [sb("hn%d" % i, [128, DH], BF16) for i in range(2)]
    kw = [sb("kw%d" % i, [128, DH], BF16) for i in range(2)]
    csm = [sb("csm%d" % i, [128, 32]) for i in range(2)]
    sig = hn; ysq = kw; lt1 = hh
    lt2 = [q[:].rearrange("p c t -> p (c t)") for q in qs]
    ssm = sb("ssm", [NS, 64])
    qtok = ovl[0:NS, 2048:2560]
    ntok = ovl[0:NS, 0:1024].bitcast(F32); hs = ovl[0:NS, 1024:2048].bitcast(F32)
    dexp = sb("dexp", [NS, 64]); dec_bc = sb("dec_bc", [128, 64])
    kwm = [ovl[0:NS, 2560 + 512 * i:3072 + 512 * i] for i in range(2)]
    ps = [nc.alloc_psum_tensor("ps%d" % i, [128, 512], F32) for i in range(8)]
    print("sbuf bytes remaining:", nc.sbuf_bytes_remaining)
    wshared = {"descs": []}

    for dry in (True, False):
        try:
            emit(nc, dry, locals(), wshared)
        except _Stop:
            T_ = wshared["T"]
            T_.finish()
            if not dry:
                print("STOP counts:", {e.name: e.cnt for e in (T_.pe, T_.dve, T_.act, T_.pool, T_.sp)}, {k: [x[1] for x in v] for k, v in T_.dsem.items()})
    return nc


class _Stop(Exception):
    pass


def emit(nc, dry, L, ws):
    g = dict(L)
    T = Trk(nc, dry)
    ws["T"] = T
    import os as _os
    _stop = int(_os.environ.get("K_STOP", "0"))

    def stage(n):
        if _stop == n:
            raise _Stop()
    PE, DVE, ACT, POOL, SP = T.pe, T.dve, T.act, T.pool, T.sp
    V, Aeng, Pm, G = nc.vector, nc.scalar, nc.tensor, nc.gpsimd
    xp, xs, cc, sC, sn, sm, smc, scc = (g[k] for k in ["xp", "xs", "cc", "sC", "sn", "sm", "smc", "scc"])
    pcol, pcol_d = g["pcol"], g["pcol_d"]
    ps = g["ps"]
    bufs = {}

    def B(name):
        if name not in bufs:
            bufs[name] = Buf(name)
        return bufs[name]

    def op(Eg, reads, writes, fn):
        T.op(Eg, [B(r) if isinstance(r, str) else r for r in reads], [B(w) if isinstance(w, str) else w for w in writes], fn)

    def dma(Q, reads, writes, fn):
        T.dma(Q, [B(r) if isinstance(r, str) else r for r in reads], [B(w) if isinstance(w, str) else w for w in writes], fn)

    def handoff(olds, news):
        T.handoff([B(o) for o in olds], [B(n) for n in news])

    def pc(name, c, n=1):
        o = PCOL[name] + c
        return pcol[:, o:o + n]

    HL = ["szh", "qTh", "kTh", "vTh"]

    pst = {"free": list(range(8)), "i": 0}

    def bank():
        i = pst["free"][pst["i"] % len(pst["free"])]
        pst["i"] += 1
        return ps[i], B("ps%d" % i)

    def reserve(n):
        got = pst["free"][-n:]
        pst["free"] = pst["free"][:-n]
        return [(ps[i], B("ps%d" % i)) for i in got], got

    def release(got):
        pst["free"] = pst["free"] + got

    wstate = {"i": 0, "issued": 0}
    PREF = 2
    wsl = g["wsl"]

    wcache = g["wcache"]
    wc = {"map": {}, "n": 0}

    def w_issue(j):
        src, kc, ncols, key = ws["descs"][j]
        slot = j % 3
        n = kc * ncols
        dst = wsl[slot][:, 0:n].rearrange("p (k n) -> p k n", k=kc)
        if WCACHE and key is not None and key in wc["map"]:
            ci = wc["map"][key]
            dma(POOL, ["wc%d" % ci], ["wsl%d" % slot], lambda: G.dma_start(out=wsl[slot][:, 0:n], in_=wcache[ci, :, 0:n]))
            return
        dma(POOL, [], ["wsl%d" % slot], lambda: G.dma_start(out=dst, in_=src.rearrange("(k p) n -> p k n", p=128)))
        if WCACHE and key is not None and ws["uses"].get(key, 0) > 1 and wc["n"] < NWC:
            ci = wc["n"]
            wc["n"] += 1
            wc["map"][key] = ci
            dma(POOL, ["wsl%d" % slot], ["wc%d" % ci], lambda: G.dma_start(out=wcache[ci, :, 0:n], in_=wsl[slot][:, 0:n]))

    def wget(src, kc, ncols, key=None):
        i = wstate["i"]
        wstate["i"] += 1
        slot = i % 3
        view = wsl[slot][:, 0:kc * ncols].rearrange("p (k n) -> p k n", k=kc)
        if dry:
            ws["descs"].append((src, kc, ncols, key))
            if key is not None:
                ws.setdefault("uses", {})
                ws["uses"][key] = ws["uses"].get(key, 0) + 1
            return view, B("wsl%d" % slot)
        while wstate["issued"] < min(len(ws["descs"]), i + PREF + 1):
            w_issue(wstate["issued"])
            wstate["issued"] += 1
        return view, B("wsl%d" % slot)

    ident, identf, onesf, tri, maskb, maskT, onesE, onesb, eye16, sel16 = (
        g[k] for k in ["ident", "identf", "onesf", "tri", "maskb", "maskT", "onesE", "onesb", "eye16", "sel16"])
    sm1 = g["sm1"]

    op(POOL, [], ["identf"], lambda: G.memset(identf[:], 0.0))
    op(POOL, [], ["identf"], lambda: G.affine_select(out=identf[:], in_=identf[:], pattern=[[-1, 128]], compare_op=ALU.not_equal,
                                                     fill=1.0, base=0, channel_multiplier=1))
    op(DVE, ["identf"], ["ident"], lambda: V.tensor_copy(ident[:], identf[:]))
    op(POOL, [], ["onesf"], lambda: G.memset(onesf[:], 1.0))
    op(POOL, [], ["onesE"], lambda: G.memset(onesE[:], 1.0 / E))
    op(POOL, [], ["onesb"], lambda: G.memset(onesb[:], 1.0))
    op(POOL, [], ["tri"], lambda: G.memset(tri[:], 1.0))
    op(POOL, [], ["tri"], lambda: G.affine_select(out=tri[:], in_=tri[:], pattern=[[1, 128]], compare_op=ALU.is_ge, fill=0.0,
                                                  base=0, channel_multiplier=-1))
    op(POOL, [], ["maskb"], lambda: G.memset(maskb[:], 0.0))
    op(POOL, [], ["maskb"], lambda: G.affine_select(out=maskb[:], in_=maskb[:], pattern=[[-1, 128]], compare_op=ALU.is_ge, fill=-BIG,
                                                    base=0, channel_multiplier=1))
    op(POOL, [], ["maskT"], lambda: G.memset(maskT[:], 0.0))
    op(POOL, [], ["maskT"], lambda: G.affine_select(out=maskT[:], in_=maskT[:], pattern=[[1, 128]], compare_op=ALU.is_ge, fill=BIG,
                                                    base=0, channel_multiplier=-1))
    op(POOL, [], ["eye16"], lambda: G.memset(eye16[:], 1.0))
    op(POOL, [], ["eye16"], lambda: G.affine_select(out=eye16[:], in_=eye16[:], pattern=[[1, 16], [-1, 16]], compare_op=ALU.is_equal,
                                                    fill=0.0, base=0, channel_multiplier=0))
    op(POOL, [], ["sel16"], lambda: G.memset(sel16[:], 1.0))
    op(POOL, [], ["sel16"], lambda: G.affine_select(out=sel16[:], in_=sel16[:], pattern=[[0, 128]], compare_op=ALU.is_ge, fill=0.0,
                                                    base=-NS, channel_multiplier=1))
    dma(SP, [], ["pcol"], lambda: nc.sync.dma_start(out=pcol[:], in_=pcol_d))
    fg_bc, bgt_bc, wg = g["fg_bc"], g["bgt_bc"], g["wg"]
    dma(SP, [], ["fg_bc"], lambda: nc.sync.dma_start(out=fg_bc[:], in_=g["final_g"].partition_broadcast(128)))
    dma(SP, [], ["bgt_bc"], lambda: nc.sync.dma_start(out=bgt_bc[:], in_=g["b_gates"].partition_broadcast(128)))
    dma(POOL, [], ["wg"], lambda: G.dma_start(out=wg[:], in_=g["w_gates"].rearrange("(k p) n -> p k n", p=128)))

    siluT, modT, gsc, gate_t = g["siluT"], g["modT"], g["gsc"], g["gate_t"]
    xres, xn = g["xres"], g["xn"]
    R = NS + 1
    dma(SP, [], ["xres"], lambda: nc.sync.dma_start(out=xres[0:R, 0, :], in_=cc))
    op(ACT, ["xres"], ["xres"], lambda: Aeng.activation(out=xres[0:R, 1, :], in_=xres[0:R, 0, :], func=AF.Silu))
    pt, pb = bank()
    for c in range(8):
        op(PE, ["xres", "identf"], [pb], lambda c=c, pt=pt: Pm.transpose(pt[:, c * 32:c * 32 + R], xres[0:R, 1, c * 128:(c + 1) * 128],
                                                                       identf[0:R, 0:R]))
    op(DVE, [pb], ["siluT"], lambda pt=pt: V.tensor_copy(siluT[:], pt[:, 0:256].rearrange("p (c r) -> p c r", c=8)[:, :, 0:R]))
    bg_rows = xres[0:R, 2, :]
    for l in range(2):
        dma(SP, ["xres"], ["xres"], lambda l=l: nc.sync.dma_start(out=bg_rows, in_=g["b_gate"][l].partition_broadcast(R)))
        for p in range(6):
            wv, wb = wget(g["w_ada"][l][:, p * 512:(p + 1) * 512], 8, 512)
            pt, pb = bank()

            def mm(wv=wv, pt=pt):
                ins = None
                for fc in range(4):
                    for kc in range(8):
                        ins = Pm.matmul(pt[:, fc * 32:fc * 32 + R], lhsT=wv[:, kc, fc * 128:(fc + 1) * 128], rhs=siluT[:, kc, :],
                                        start=(kc == 0), stop=(kc == 7))
                return ins
            op(PE, [wb, "siluT"], [pb], mm)
            for fc in range(4):
                f = 4 * p + fc
                op(ACT, [pb, "pcol"], ["modT%d" % l], lambda f=f, fc=fc, pt=pt, l=l: Aeng.activation(
                    out=modT[l][:, f, :], in_=pt[:, fc * 32:fc * 32 + R], func=AF.Identity, bias=pc("b_ada%d" % l, f)))
            if p >= 4:
                pt2, pb2 = bank()

                def mm2(wv=wv, pt2=pt2):
                    ins = None
                    for kc in range(8):
                        ins = Pm.matmul(pt2[0:R, :], lhsT=siluT[:, kc, :], rhs=wv[:, kc, :], start=(kc == 0), stop=(kc == 7))
                    return ins
                op(PE, [wb, "siluT"], [pb2], mm2)
                cs = slice((p - 4) * 512, (p - 3) * 512)
                op(DVE, [pb2, "xres"], ["gate_t%d" % l], lambda pt2=pt2, cs=cs, l=l: V.tensor_tensor(
                    out=gate_t[l][0:R, cs], in0=pt2[0:R, :], in1=bg_rows[:, cs], op=ALU.add))
        op(DVE, ["modT%d" % l], ["gsc%d" % l], lambda l=l: V.tensor_scalar(
            out=gsc[l][:], in0=modT[l][:, 8:16, :], scalar1=1.0, scalar2=None, op0=ALU.add))
        op(DVE, ["gsc%d" % l, "pcol"], ["gsc%d" % l], lambda l=l: V.tensor_tensor(
            out=gsc[l][:], in0=gsc[l][:], in1=pc("norm_g%d" % l, 0, 8).unsqueeze(2).to_broadcast([128, 8, R]), op=ALU.mult))

    Gf, Gbf = g["Gf"], g["Gbf"]
    WTs = g["yT"]
    for hq in range(H):
        for wi, wd in enumerate([g["ml_w_q"], g["ml_w_k"], g["ml_w_v"]]):
            wv, wb = wget(wd[hq], 4, 512, key=("qkv", wi, hq))
            for ec in range(4):
                pt, pb = bank()
                ptb = pt[:].bitcast(BF16)
                for kc in range(4):
                    op(PE, [wb, "ident"], [pb], lambda kc=kc, ec=ec, ptb=ptb, wv=wv: Pm.transpose(
                        ptb[:, kc * 128:(kc + 1) * 128], wv[:, kc, ec * 128:(ec + 1) * 128], ident[:]))
                op(DVE, [pb], ["WTs"], lambda ec=ec, ptb=ptb: V.tensor_copy(WTs[:, ec, :], ptb[:, 0:512]))
            pt2, pb2 = bank()

            def mmf(pt2=pt2, wi=wi, hq=hq):
                ins = None
                for kc in range(4):
                    for ec in range(4):
                        ins = Pm.matmul(pt2[:, kc * 8:(kc + 1) * 8], lhsT=WTs[:, ec, kc * 128:(kc + 1) * 128],
                                        rhs=wg[:, wi * 16 + hq * 4 + ec, :], start=(ec == 0), stop=(ec == 3))
                return ins
            op(PE, ["WTs", "wg"], [pb2], mmf)
            dstG = Gf[:, 1 if wi == 2 else 0, 4 * hq:4 * hq + 4, :]
            src = pt2[:, 0:32].rearrange("p (k g) -> p k g", k=4)
            if wi == 1:
                op(DVE, [pb2, "Gf"], ["Gf"], lambda dstG=dstG, src=src: V.tensor_tensor(out=dstG, in0=dstG, in1=src, op=ALU.add))
            else:
                op(DVE, [pb2], ["Gf"], lambda dstG=dstG, src=src: V.tensor_copy(dstG, src))
    op(DVE, ["Gf"], ["Gbf"], lambda: V.tensor_copy(Gbf[:], Gf[:]))
    handoff(["WTs"], ["yT"] + HL)

    stage(1)
    hT, A1, A2, yT = g["hT"], g["A1"], g["A2"], g["yT"]
    hh, hn, kw, qs, csm = g["hh"], g["hn"], g["kw"], g["qs"], g["csm"]
    sig, ysq, lt1, lt2 = g["sig"], g["ysq"], g["lt1"], g["lt2"]
    rstd_bc, nmr_bc = g["rstd_bc"], g["nmr_bc"]

    xnb = [g["xn"], g["xn2"]]
    rr = {"n": 0}

    def rms_rows(src, nt, srcbuf="xres"):
        par = rr["n"] % 2
        rr["n"] += 1
        sn_ = "sm1_%d" % par
        c0 = par * 16
        st = sm1[0:nt, c0:c0 + 16]
        op(DVE, [srcbuf], [sn_], lambda: V.bn_stats(out=st[:, 0:6], in_=src[:, 0:512]))
        op(DVE, [srcbuf, sn_], [sn_], lambda: V.bn_stats(out=st[:, 6:12], in_=src[:, 512:1024]))
        op(DVE, [sn_], [sn_], lambda: V.bn_aggr(out=st[:, 12:14], in_=st[:, 0:12]))
        op(DVE, [sn_], [sn_], lambda: V.scalar_tensor_tensor(out=st[:, 14:15], in0=st[:, 12:13], scalar=st[:, 12:13], in1=st[:, 13:14],
                                                            op0=ALU.mult, op1=ALU.add))
        op(ACT, [sn_], [sn_], lambda: Aeng.activation(out=st[:, 15:16], in_=st[:, 14:15], func=AF.Ln, bias=EPS))
        op(ACT, [sn_], [sn_], lambda: Aeng.activation(out=st[:, 15:16], in_=st[:, 15:16], func=AF.Exp, scale=-0.5))
        return st[:, 15:16], sn_

    def norm_mod(l, ntiles, nt, is_s, stage_src=None):
        for tt in range(ntiles):
            if stage_src is not None:
                dma(SP, [], ["xst"], lambda tt=tt: nc.sync.dma_start(out=g["xst"][:, :], in_=stage_src(tt)))
                srcx, sbn = g["xst"][0:nt, :], "xst"
            else:
                srcx, sbn = xres[0:nt, tt, :], "xres"
            rs, sn_ = rms_rows(srcx, nt, sbn)
            xb = xnb[tt % 2]
            xbn = "xn%d" % (tt % 2)
            op(ACT, [sbn, sn_], [xbn], lambda xb=xb, rs=rs, srcx=srcx: Aeng.activation(out=xb[0:nt, :], in_=srcx, func=AF.Copy, scale=rs))
            pt, pb = bank()
            ptb = pt[:].bitcast(BF16).rearrange("p (c t) -> p c t", c=8)
            for c in range(8):
                op(PE, [xbn, "ident"], [pb], lambda c=c, ptb=ptb, xb=xb: Pm.transpose(ptb[:, c, 0:nt], xb[0:nt, c * 128:(c + 1) * 128],
                                                                                   ident[0:nt, 0:nt]))
            for c in range(8):
                dst = hT[:, c, tt * 128:tt * 128 + nt]
                if is_s:
                    tmp = sm1[:, 32:32 + NS]
                    op(DVE, [pb, "gsc%d" % l], ["sm1b"], lambda c=c, ptb=ptb, tmp=tmp: V.tensor_tensor(
                        out=tmp, in0=ptb[:, c, 0:nt], in1=gsc[l][:, c, 0:NS], op=ALU.mult))
                    op(DVE, ["sm1b", "modT%d" % l], ["hT"], lambda c=c, dst=dst, tmp=tmp: V.tensor_tensor(
                        out=dst, in0=tmp, in1=modT[l][:, c, 0:NS], op=ALU.add))
                else:
                    op(DVE, [pb, "gsc%d" % l, "modT%d" % l], ["hT"], lambda c=c, dst=dst, ptb=ptb: V.tensor_scalar(
                        out=dst, in0=ptb[:, c, 0:nt], scalar1=gsc[l][:, c, NS:NS + 1], scalar2=modT[l][:, c, NS:NS + 1],
                        op0=ALU.mult, op1=ALU.add))

    def fm_group(wv, wb, srcf, src_bufs, KC, N, nfc, evac):
        for fc in range(nfc):
            pt, pb = bank()

            def mm(fc=fc, pt=pt):
                ins = None
                for kc in range(KC):
                    ins = Pm.matmul(pt[:, 0:N], lhsT=wv[:, kc, fc * 128:(fc + 1) * 128], rhs=srcf(kc), start=(kc == 0), stop=(kc == KC - 1))
                return ins
            op(PE, [wb] + src_bufs, [pb], mm)
            evac(fc, pt, pb)

    def tm_group(wv, wb, lhs_f, src_bufs, KC, nt, ncols, evac):
        pt, pb = bank()

        def mm():
            ins = None
            for kc in range(KC):
                ins = Pm.matmul(pt[0:nt, 0:ncols], lhsT=lhs_f(kc), rhs=wv[:, kc, 0:ncols], start=(kc == 0), stop=(kc == KC - 1))
            return ins
        op(PE, [wb] + src_bufs, [pb], mm)
        evac(pt, pb)

    def rows_out(srcf, src_bufs, Rr, dstf):
        for q in range(4):
            pt, pb = bank()
            ptb = pt[:].bitcast(BF16)
            for cc_ in range(4):
                c = q * 4 + cc_
                op(PE, src_bufs + ["ident"], [pb], lambda c=c, cc_=cc_, ptb=ptb: Pm.transpose(ptb[0:Rr, cc_ * 128:(cc_ + 1) * 128], srcf(c),
                                                                                         ident[:, :]))
            st = hh[q % 2]
            op(DVE, [pb], ["hh%d" % (q % 2)], lambda st=st, ptb=ptb: V.tensor_copy(st[0:Rr, :], ptb[0:Rr, 0:512]))
            dma(SP, ["hh%d" % (q % 2)], [], lambda st=st, q=q: nc.sync.dma_start(out=dstf(q), in_=st[0:Rr, :]))

    def outproj_resid(l, wdram, ntiles, nt):
        for hf in range(2):
            banks, got = reserve(ntiles)
            for kh in range(2):
                wv, wb = wget(wdram[kh * 1024:(kh + 1) * 1024, hf * 512:(hf + 1) * 512], 8, 512, key=("wout", l, kh, hf))
                for tt in range(ntiles):
                    pt, pb = banks[tt]

                    def mm(tt=tt, pt=pt, wv=wv, kh=kh):
                        ins = None
                        for kc in range(8):
                            ins = Pm.matmul(pt[0:nt, :], lhsT=A2[:, kh * 8 + kc, tt * 128:tt * 128 + nt], rhs=wv[:, kc, :],
                                            start=(kh == 0 and kc == 0), stop=(kh == 1 and kc == 7))
                        return ins
                    op(PE, [wb, "A2"], [pb], mm)
            for tt in range(ntiles):
                pt, pb = banks[tt]
                gt = gate_t[l][0:nt, hf * 512:(hf + 1) * 512]
                xr = xres[0:nt, tt, hf * 512:(hf + 1) * 512]
                lt = hh[tt % 2]
                op(DVE, [pb, "gate_t%d" % l], ["hh%d" % (tt % 2)], lambda pt=pt, gt=gt, lt=lt: V.tensor_tensor(
                    out=lt[0:nt, :], in0=pt[0:nt, :], in1=gt, op=ALU.mult))
                op(DVE, ["hh%d" % (tt % 2), "xres"], ["xres"], lambda xr=xr, lt=lt: V.tensor_tensor(out=xr, in0=xr, in1=lt[0:nt, :], op=ALU.add))
            release(got)

    szh, qTh, kTh, vTh, ktok, vtok, qtok = (g[k] for k in ["szh", "qTh", "kTh", "vTh", "ktok", "vtok", "qtok"])
    C32, Cbf, n32, nbf, mprev = (g[k] for k in ["C32", "Cbf", "n32", "nbf", "mprev"])
    igt, lft, gpre, bcol, btot, gcol = (g[k] for k in ["igt", "lft", "gpre", "bcol", "btot", "gcol"])
    mhist, chist = g["mhist"], g["chist"]

    def layer0_front(ntiles, nt, is_s, stride, skip_norm=False):
        N = ntiles * nt
        if not skip_norm:
            norm_mod(0, ntiles, nt, is_s)
        newc = slice(3 * stride, 3 * stride + N)
        for p in range(4):
            wv, wb = wget(g["ml_w_in"][:, p * 512:(p + 1) * 512], 8, 512, key=("win", p))

            def ev(fc, pt, pb, p=p):
                op(ACT, [pb], ["A1"], lambda: Aeng.activation(out=A1[:, 4 * p + fc, newc], in_=pt[:, 0:N], func=AF.Copy))
            fm_group(wv, wb, lambda kc: hT[:, kc, 0:N], ["hT"], 8, N, 4, ev)
        dg4 = g["dg4"]
        for c in range(16):
            dg = dg4[c % 2]
            op(POOL, ["ident", "pcol"], ["dg4_%d" % (c % 2)], lambda c=c, dg=dg: G.tensor_tensor(
                out=dg[:], in0=ident[:].unsqueeze(1).to_broadcast([128, 4, 128]),
                in1=pc("ml_w_convT", 4 * c, 4).unsqueeze(2).to_broadcast([128, 4, 128]), op=ALU.mult))
            pt, pb = bank()

            def mm(c=c, dg=dg, pt=pt):
                ins = None
                for j in range(4):
                    ins = Pm.matmul(pt[:, 0:N], lhsT=dg[:, j, :], rhs=A1[:, c, j * stride:j * stride + N], start=(j == 0), stop=(j == 3))
                return ins
            op(PE, ["dg4_%d" % (c % 2), "A1"], [pb], mm)
            op(ACT, [pb, "pcol"], ["A2"], lambda c=c, pt=pt: Aeng.activation(out=A2[:, c, 0:N], in_=pt[:, 0:N], func=AF.Silu,
                                                                           bias=pc("ml_b_conv", c)))
        handoff(["yT"], HL)
        gbanks, ggot = reserve(ntiles)
        for tt in range(ntiles):
            gpt, gpb = gbanks[tt]

            def mmgt(tt=tt, gpt=gpt):
                ins = None
                for ch in range(16):
                    ins = Pm.matmul(gpt[0:nt, 0:8], lhsT=A2[:, ch, tt * 128:tt * 128 + nt], rhs=Gbf[:, 0, ch, :], start=(ch == 0), stop=False)
                for ch in range(16):
                    c0 = 3 * stride + tt * 128
                    ins = Pm.matmul(gpt[0:nt, 0:8], lhsT=A1[:, ch, c0:c0 + nt], rhs=Gbf[:, 1, ch, :], start=False, stop=(ch == 15))
                return ins
            op(PE, ["A2", "A1", "Gbf"], [gpb], mmgt)
        for tt in range(ntiles):
            gpt, gpb = gbanks[tt]
            op(DVE, [gpb, "bgt_bc"], ["gpre"], lambda tt=tt, gpt=gpt: V.tensor_tensor(out=gpre[0:nt, :], in0=gpt[0:nt, 0:8],
                                                                                  in1=bgt_bc[0:nt, :], op=ALU.add))
            op(DVE, ["gpre"], ["igt"], lambda tt=tt: V.tensor_copy(igt[0:nt, tt, :], gpre[0:nt, 0:4]))
            op(ACT, ["gpre"], ["gpre"], lambda: Aeng.activation(out=gpre[0:nt, 4:8], in_=gpre[0:nt, 4:8], func=AF.Exp, scale=-1.0))
            op(ACT, ["gpre"], ["gpre"], lambda: Aeng.activation(out=gpre[0:nt, 4:8], in_=gpre[0:nt, 4:8], func=AF.Ln, bias=1.0))
            op(DVE, ["gpre"], ["lft"], lambda tt=tt: V.tensor_scalar(out=lft[0:nt, tt, :], in0=gpre[0:nt, 4:8], scalar1=-1.0, scalar2=None,
                                                                   op0=ALU.mult))
        release(ggot)

    def head_proj(h, ntiles, nt, N, stride, is_s, between=None):
        bt = between or (lambda i: None)
        wv, wb = wget(g["ml_w_in"][:, E + h * 512:E + (h + 1) * 512], 8, 512, key=("win", 4 + h))

        def evz(fc, pt, pb):
            op(ACT, [pb], ["szh"], lambda: Aeng.activation(out=szh[:, fc, 0:N], in_=pt[:, 0:N], func=AF.Silu))
        bt(0)
        fm_group(wv, wb, lambda kc: hT[:, kc, 0:N], ["hT"], 8, N, 4, evz)
        bt(1)
        wv, wb = wget(g["ml_w_q"][h], 4, 512, key=("qkv", 0, h))

        def evq(fc, pt, pb):
            op(DVE, [pb], ["qTh"], lambda: V.tensor_copy(qTh[:, fc, 0:N], pt[:, 0:N]))
        fm_group(wv, wb, lambda kc: A2[:, 4 * h + kc, 0:N], ["A2"], 4, N, 4, evq)
        if is_s:
            def evqt(pt, pb):
                op(DVE, [pb], ["qtok"], lambda: V.tensor_copy(qtok[0:nt, :], pt[0:nt, :]))
            tm_group(wv, wb, lambda kc: A2[:, 4 * h + kc, 0:nt], ["A2"], 4, nt, 512, evqt)
        bt(2)
        wv, wb = wget(g["ml_w_k"][h], 4, 512, key=("qkv", 1, h))
        ksc = float(DH) ** -0.5

        def evk(fc, pt, pb):
            op(ACT, [pb], ["kTh"], lambda: Aeng.activation(out=kTh[:, fc, 0:N], in_=pt[:, 0:N], func=AF.Copy, scale=ksc))
        fm_group(wv, wb, lambda kc: A2[:, 4 * h + kc, 0:N], ["A2"], 4, N, 4, evk)
        for tt in range(ntiles):
            def evkt(pt, pb, tt=tt):
                dst = ktok[0:nt, h, :] if is_s else ktok[0:nt, tt, :]
                op(ACT, [pb], ["ktok"], lambda: Aeng.activation(out=dst, in_=pt[0:nt, :], func=AF.Copy, scale=ksc))
            tm_group(wv, wb, lambda kc, tt=tt: A2[:, 4 * h + kc, tt * 128:tt * 128 + nt], ["A2"], 4, nt, 512, evkt)
        bt(3)
        wv, wb = wget(g["ml_w_v"][h], 4, 512, key=("qkv", 2, h))
        for tt in range(ntiles):
            def evvt(pt, pb, tt=tt):
                dst = vtok[0:nt, h, :] if is_s else vtok[0:nt, tt, :]
                op(DVE, [pb], ["vtok"], lambda: V.tensor_copy(dst, pt[0:nt, :]))
            tm_group(wv, wb, lambda kc, tt=tt: A1[:, 4 * h + kc, 3 * stride + tt * 128:3 * stride + tt * 128 + nt], ["A1"], 4, nt, 512, evvt)

    def scale_xc_skip(h, N):
        for c in range(4):
            ch = 4 * h + c
            op(DVE, ["A2", "pcol"], ["A2"], lambda ch=ch: V.tensor_scalar(out=A2[:, ch, 0:N], in0=A2[:, ch, 0:N], scalar1=pc("ml_skip", ch),
                                                                      scalar2=None, op0=ALU.mult))

    def ln_rows(par, nt, src_hh, src_buf, dst_hn):
        cs_ = csm[par]
        cn = "csm%d" % par
        op(DVE, [src_buf], [cn], lambda: V.bn_stats(out=cs_[0:nt, 0:6], in_=src_hh))
        op(DVE, [cn], [cn], lambda: V.bn_aggr(out=cs_[0:nt, 6:8], in_=cs_[0:nt, 0:6]))
        op(ACT, [cn], [cn], lambda: Aeng.activation(out=cs_[0:nt, 8:9], in_=cs_[0:nt, 7:8], func=AF.Ln, bias=EPS))
        op(ACT, [cn], [cn], lambda: Aeng.activation(out=cs_[0:nt, 8:9], in_=cs_[0:nt, 8:9], func=AF.Exp, scale=-0.5))
        op(DVE, [cn], [cn], lambda: V.tensor_scalar(out=cs_[0:nt, 9:10], in0=cs_[0:nt, 6:7], scalar1=cs_[0:nt, 8:9], scalar2=-1.0,
                                                   op0=ALU.mult, op1=ALU.mult))
        op(ACT, [src_buf, cn], ["hn%d" % par], lambda: Aeng.activation(out=dst_hn, in_=src_hh, func=AF.Identity,
                                                                      scale=cs_[0:nt, 8:9], bias=cs_[0:nt, 9:10]))

    WtAll, wintAll, clampAll, wkAll, decayAll, mtmp, St = (g[k] for k in ["WtAll", "wintAll", "clampAll", "wkAll", "decayAll", "mtmp", "St"])

    def psb(i):
        return ps[i], B("ps%d" % i)

    def mchain(tt):
        dgG = hh[0][:].rearrange("p (h s) -> p h s", h=4)
        dgM = hh[1][:].rearrange("p (h s) -> p h s", h=4)
        pt, pb = bank()

        def mm(pt=pt):
            Pm.matmul(pt[:, 0:4], lhsT=tri[:], rhs=lft[:, tt, :], start=True, stop=True)
            return Pm.matmul(pt[:, 4:8], lhsT=onesf[:], rhs=lft[:, tt, :], start=True, stop=True)
        op(PE, ["tri", "onesf", "lft"], [pb], mm)
        op(DVE, [pb], ["bcol"], lambda pt=pt: V.tensor_copy(bcol[:, tt, :], pt[:, 0:4]))
        op(DVE, [pb], ["btot"], lambda pt=pt: V.tensor_copy(btot[:, tt, :], pt[:, 4:8]))
        op(DVE, ["igt", "bcol"], ["gcol"], lambda: V.tensor_tensor(out=gcol[:, tt, :], in0=igt[:, tt, :], in1=bcol[:, tt, :], op=ALU.subtract))
        op(DVE, ["identf", "gcol"], ["hh0"], lambda: V.tensor_tensor(
            out=dgG, in0=identf[:].unsqueeze(1).to_broadcast([128, 4, 128]), in1=gcol[:, tt, :].unsqueeze(2).to_broadcast([128, 4, 128]),
            op=ALU.mult))
        ptG, pbG = bank()

        def mmG(ptG=ptG):
            Pm.matmul(ptG[:, :], lhsT=onesf[:], rhs=hh[0][:], start=True, stop=False)
            ins = None
            for hq in range(4):
                ins = Pm.matmul(ptG[:, hq * 128:(hq + 1) * 128], lhsT=identf[:], rhs=maskb[:], start=False, stop=(hq == 3))
            return ins
        op(PE, ["hh0", "onesf", "identf", "maskb"], [pbG], mmG)
        op(DVE, [pbG], ["mtmp"], lambda ptG=ptG: V.tensor_reduce(out=mtmp[:, 0:4], in_=ptG[:, :].rearrange("p (h s) -> p h s", h=4), axis=AX.X,
                                                              op=ALU.max))
        op(DVE, ["mtmp", "mprev"], ["mtmp"], lambda: V.tensor_tensor(out=mtmp[:, 4:8], in0=mtmp[:, 0:4], in1=mprev[:], op=ALU.max))
        op(DVE, ["identf", "mtmp"], ["hh1"], lambda: V.tensor_tensor(
            out=dgM, in0=identf[:].unsqueeze(1).to_broadcast([128, 4, 128]), in1=mtmp[:, 4:8].unsqueeze(2).to_broadcast([128, 4, 128]),
            op=ALU.mult))
        ptA, pbA = bank()
        ptB, pbB = bank()
        op(PE, ["hh1", "onesf"], [pbA], lambda ptA=ptA: Pm.matmul(ptA[:, :], lhsT=onesf[:], rhs=hh[1][:], start=True, stop=True))

        def mmB(ptB=ptB):
            Pm.matmul(ptB[:, :], lhsT=onesf[:], rhs=hh[1][:], start=True, stop=False)
            ins = None
            for hq in range(4):
                ins = Pm.matmul(ptB[:, hq * 128:(hq + 1) * 128], lhsT=identf[:], rhs=maskT[:], start=False, stop=(hq == 3))
            return ins
        op(PE, ["hh1", "onesf", "identf", "maskT"], [pbB], mmB)
        for hq in range(4):
            op(ACT, [pbB, "gcol"], ["WtAll"], lambda hq=hq, ptB=ptB: Aeng.activation(
                out=WtAll[:, tt, hq, :], in_=ptB[:, hq * 128:(hq + 1) * 128], func=AF.Exp, scale=-1.0, bias=gcol[:, tt, hq:hq + 1]))
        for hq in range(4):
            op(ACT, [pbA, "mprev"], ["wintAll"], lambda hq=hq, ptA=ptA: Aeng.activation(
                out=wintAll[:, tt, hq, :], in_=ptA[:, hq * 128:(hq + 1) * 128], func=AF.Exp, scale=-1.0, bias=mprev[:, hq:hq + 1]))
        op(DVE, [pbA], ["mtmp"], lambda ptA=ptA: V.tensor_copy(mtmp[:, 8:12], ptA[:, :].rearrange("p (h s) -> p h s", h=4)[:, :, 127]))
        op(DVE, ["mtmp", "bcol"], ["mtmp"], lambda: V.tensor_tensor(out=mtmp[:, 12:16], in0=mtmp[:, 4:8], in1=bcol[:, tt, :], op=ALU.add))
        op(ACT, ["mtmp"], ["clampAll"], lambda: Aeng.activation(out=clampAll[:, tt, :], in_=mtmp[:, 12:16], func=AF.Exp, scale=-1.0))
        op(DVE, ["mtmp", "gcol"], ["mtmp"], lambda: V.tensor_tensor(out=mtmp[:, 16:20], in0=gcol[:, tt, :], in1=mtmp[:, 8:12], op=ALU.subtract))
        op(ACT, ["mtmp"], ["wkAll"], lambda: Aeng.activation(out=wkAll[:, tt, :], in_=mtmp[:, 16:20], func=AF.Exp))
        op(DVE, ["mtmp", "mprev"], ["mtmp"], lambda: V.tensor_tensor(out=mtmp[:, 20:24], in0=mprev[:], in1=mtmp[:, 8:12], op=ALU.subtract))
        op(ACT, ["mtmp"], ["decayAll"], lambda: Aeng.activation(out=decayAll[:, tt, :], in_=mtmp[:, 20:24], func=AF.Exp))
        op(DVE, ["btot", "mtmp", "mprev"], ["mprev"], lambda: V.tensor_tensor(out=mprev[:], in0=btot[:, tt, :], in1=mtmp[:, 8:12], op=ALU.add))

    def prompt_cell(h, ntiles):
        Chs = ["C32_%d_%d" % (h, c) for c in range(4)]
        Cb = [Cbf, vTh]
        Cbn = ["rn", "vTh"]
        nb = [nbf, g["nbf2"]]
        nbn = ["nbf", "nbf2"]
        op(ACT, Chs, [Cbn[1]], lambda: Aeng.activation(out=Cb[1], in_=C32[:, 4 * h:4 * h + 4, :], func=AF.Copy))
        op(DVE, ["n32"], [nbn[1]], lambda: V.tensor_copy(nb[1][:], n32[:, 4 * h:4 * h + 4]))

        def F(tt):
            par = tt % 2
            tk = slice(tt * 128, (tt + 1) * 128)
            ptS, pbS = psb(2 + par)
            sn_, qn_ = "St%d" % par, "qs%d" % par

            def mmS():
                ins = None
                for c in range(4):
                    ins = Pm.matmul(ptS[:, 0:128], lhsT=kTh[:, c, tk], rhs=qTh[:, c, tk], start=(c == 0), stop=(c == 3))
                return ins
            op(PE, ["kTh", "qTh"], [pbS], mmS)
            op(DVE, [pbS, "WtAll"], [sn_], lambda: V.tensor_tensor(out=St[par][:], in0=ptS[:, 0:128], in1=WtAll[:, tt, h, :], op=ALU.mult))
            op(DVE, ["qTh", "wintAll"], [qn_], lambda: V.tensor_tensor(
                out=qs[par][:], in0=qTh[:, :, tk], in1=wintAll[:, tt, h, :].unsqueeze(1).to_broadcast([128, 4, 128]), op=ALU.mult))

        def N(tt):
            par = tt % 2
            prv = (tt + 1) % 2
            ptS, pbS = psb(2 + par)
            ptN, pbN = psb(par)
            sn_, qn_ = "St%d" % par, "qs%d" % par

            def mmN():
                Pm.matmul(ptN[:, :], lhsT=St[par][:], rhs=vtok[:, tt, :], start=True, stop=False)
                ins = None
                for c in range(4):
                    ins = Pm.matmul(ptN[:, :], lhsT=qs[par][:, c, :], rhs=Cb[prv][:, c, :], start=False, stop=(c == 3))
                return ins
            op(PE, [sn_, "vtok", qn_, Cbn[prv]], [pbN], mmN)

            def mmD():
                Pm.matmul(ptS[:, 136:137], lhsT=St[par][:], rhs=onesb[:, 0:1], start=True, stop=False)
                ins = None
                for c in range(4):
                    ins = Pm.matmul(ptS[:, 136:137], lhsT=qs[par][:, c, :], rhs=nb[prv][:, c:c + 1], start=False, stop=(c == 3))
                return ins
            op(PE, [sn_, "onesb", qn_, nbn[prv]], [pbS], mmD)

        def U(tt):
            par = tt % 2
            ptS, pbS = psb(2 + par)
            kwn = "kw%d" % par
            op(ACT, ["ktok", "wkAll"], [kwn], lambda: Aeng.activation(out=kw[par][:], in_=ktok[:, tt, :], func=AF.Copy,
                                                                     scale=wkAll[:, tt, h:h + 1]))
            for c in range(4):
                ptC, pbC = psb(4 + (tt * 4 + c) % 3)
                op(PE, [kwn, "vtok"], [pbC], lambda c=c, ptC=ptC: Pm.matmul(ptC[:, :], lhsT=kw[par][:, c * 128:(c + 1) * 128],
                                                                         rhs=vtok[:, tt, :], start=True, stop=True))
                op(DVE, [pbC, "decayAll", Chs[c]], [Chs[c]], lambda c=c, ptC=ptC: V.scalar_tensor_tensor(
                    out=C32[:, 4 * h + c, :], in0=C32[:, 4 * h + c, :], scalar=decayAll[:, tt, h:h + 1], in1=ptC[:, :], op0=ALU.mult,
                    op1=ALU.add))
                op(ACT, [Chs[c]], [Cbn[par]], lambda c=c: Aeng.activation(out=Cb[par][:, c, :], in_=C32[:, 4 * h + c, :], func=AF.Copy))

            def mmn():
                ins = None
                for c in range(4):
                    ins = Pm.matmul(ptS[:, 128 + c:129 + c], lhsT=kw[par][:, c * 128:(c + 1) * 128], rhs=onesb[:, 0:1], start=True, stop=True)
                return ins
            op(PE, [kwn, "onesb"], [pbS], mmn)
            op(DVE, [pbS, "decayAll", "n32"], ["n32"], lambda: V.scalar_tensor_tensor(
                out=n32[:, 4 * h:4 * h + 4], in0=n32[:, 4 * h:4 * h + 4], scalar=decayAll[:, tt, h:h + 1], in1=ptS[:, 128:132], op0=ALU.mult,
                op1=ALU.add))
            op(DVE, ["n32"], [nbn[par]], lambda: V.tensor_copy(nb[par][:], n32[:, 4 * h:4 * h + 4]))

        def C1(tt):
            par = tt % 2
            cs_ = csm[par]
            cn = "csm%d" % par
            ptS, pbS = psb(2 + par)
            ptN, pbN = psb(par)
            op(DVE, [pbN], [cn], lambda: V.bn_stats(out=cs_[:, 0:6], in_=ptN[:, :]))
            op(DVE, [cn], [cn], lambda: V.bn_aggr(out=cs_[:, 6:8], in_=cs_[:, 0:6]))
            op(ACT, [pbS], [cn], lambda: Aeng.activation(out=cs_[:, 15:16], in_=ptS[:, 136:137], func=AF.Abs))
            op(DVE, [cn, "clampAll"], [cn], lambda: V.tensor_tensor(out=cs_[:, 15:16], in0=cs_[:, 15:16], in1=clampAll[:, tt, h:h + 1], op=ALU.max))
            op(DVE, [cn], [cn], lambda: V.reciprocal(out=cs_[:, 16:17], in_=cs_[:, 15:16]))
            op(DVE, [cn], [cn], lambda: V.scalar_tensor_tensor(out=cs_[:, 17:18], in0=cs_[:, 7:8], scalar=cs_[:, 16:17], in1=cs_[:, 16:17],
                                                              op0=ALU.mult, op1=ALU.mult))
            op(ACT, [cn], [cn], lambda: Aeng.activation(out=cs_[:, 8:9], in_=cs_[:, 17:18], func=AF.Ln, bias=EPS))
            op(ACT, [cn], [cn], lambda: Aeng.activation(out=cs_[:, 8:9], in_=cs_[:, 8:9], func=AF.Exp, scale=-0.5))

        def C2(tt):
            par = tt % 2
            cs_ = csm[par]
            cn = "csm%d" % par
            ptN, pbN = psb(par)
            hnn = "hn%d" % par
            op(DVE, [cn], [cn], lambda: V.tensor_tensor(out=cs_[:, 18:19], in0=cs_[:, 16:17], in1=cs_[:, 8:9], op=ALU.mult))
            op(DVE, [cn], [cn], lambda: V.scalar_tensor_tensor(out=cs_[:, 19:20], in0=cs_[:, 6:7], scalar=-1.0, in1=cs_[:, 18:19],
                                                              op0=ALU.mult, op1=ALU.mult))
            op(ACT, [pbN, cn], [hnn], lambda: Aeng.activation(out=hn[par][:], in_=ptN[:, :], func=AF.Identity, scale=cs_[:, 18:19],
                                                             bias=cs_[:, 19:20]))
            ptT, pbT = psb(7)
            ptTb = ptT[:].bitcast(BF16).rearrange("p (c t) -> p c t", c=8)
            for c in range(4):
                op(PE, [hnn, "ident"], [pbT], lambda c=c: Pm.transpose(ptTb[:, c, :], hn[par][:, c * 128:(c + 1) * 128], ident[:]))

        def C3(tt):
            tk = slice(tt * 128, (tt + 1) * 128)
            ptT, pbT = psb(7)
            ptTb = ptT[:].bitcast(BF16).rearrange("p (c t) -> p c t", c=8)
            for c in range(4):
                ch = 4 * h + c
                op(DVE, [pbT, "pcol", "A2"], ["A2"], lambda c=c, ch=ch: V.scalar_tensor_tensor(
                    out=A2[:, ch, tk], in0=ptTb[:, c, :], scalar=pc("ml_ln_g", ch), in1=A2[:, ch, tk], op0=ALU.mult, op1=ALU.add))
            op(DVE, ["A2", "szh"], ["A2"], lambda: V.tensor_tensor(out=A2[:, 4 * h:4 * h + 4, tk], in0=A2[:, 4 * h:4 * h + 4, tk],
                                                               in1=szh[:, :, tk], op=ALU.mult))

        for tt in range(ntiles + 2):
            if tt < ntiles:
                F(tt)
                U(tt)
                N(tt)
            if 0 <= tt - 2 < ntiles:
                C3(tt - 2)
            if 0 <= tt - 1 < ntiles:
                C2(tt - 1)
            if tt < ntiles:
                C1(tt)

    def layer1(ntiles, nt, is_s, stride, before_outproj=None):
        N = ntiles * nt
        dg31 = g["dg31"]
        norm_mod(1, ntiles, nt, is_s)
        newc = slice(30 * stride, 30 * stride + N)
        for p in range(4):
            wa, wab = wget(g["cf_w_in"][:, p * 512:(p + 1) * 512], 8, 512, key=("cfa", p))
            abanks, agot = reserve(4)
            for fc in range(4):
                pa, pab = abanks[fc]

                def mma(pa=pa, fc=fc, wa=wa):
                    ins = None
                    for kc in range(8):
                        ins = Pm.matmul(pa[:, 0:N], lhsT=wa[:, kc, fc * 128:(fc + 1) * 128], rhs=hT[:, kc, 0:N], start=(kc == 0), stop=(kc == 7))
                    return ins
                op(PE, [wab, "hT"], [pab], mma)
            wgt, wgb = wget(g["cf_w_in"][:, E + p * 512:E + (p + 1) * 512], 8, 512, key=("cfg", p))
            for fc in range(4):
                ch = 4 * p + fc
                pa, pab = abanks[fc]
                pg, pgb = bank()

                def mmg(pg=pg, fc=fc, wgt=wgt):
                    ins = None
                    for kc in range(8):
                        ins = Pm.matmul(pg[:, 0:N], lhsT=wgt[:, kc, fc * 128:(fc + 1) * 128], rhs=hT[:, kc, 0:N], start=(kc == 0), stop=(kc == 7))
                    return ins
                op(PE, [wgb, "hT"], [pgb], mmg)
                a = ch % 2
                op(ACT, [pgb, "pcol"], ["hn%d" % a], lambda pg=pg, a=a, ch=ch: Aeng.activation(
                    out=sig[a][:, 0:N], in_=pg[:, 0:N], func=AF.Sigmoid, bias=pc("cf_b_in", 16 + ch)))
                udst = A2[:, ch, 496:512] if is_s else A1[:, ch, newc]
                op(DVE, [pab, "pcol", "hn%d" % a], ["A2h" if is_s else "A1"], lambda pa=pa, a=a, ch=ch, udst=udst: V.scalar_tensor_tensor(
                    out=udst, in0=pa[:, 0:N], scalar=pc("cf_b_in", ch), in1=sig[a][:, 0:N], op0=ALU.add, op1=ALU.mult))
            release(agot)
        handoff(HL, ["yT"])
        sbanks, sgot = reserve(2)
        (pmean, pmeanb), (pex2, pex2b) = sbanks
        GRP = [(0, 11, DVE, V), (11, 10, POOL, G), (21, 10, DVE, V)]
        for c in range(16):
            if is_s:
                tmpc = hh[0][:, 0:496]
                yv = hh[1][:, 0:NS]
                op(DVE, ["A2h", "pcol"], ["hh0"], lambda c=c: V.tensor_tensor(
                    out=tmpc.rearrange("p (j s) -> p j s", j=31), in0=A2[:, c, 16:512].rearrange("p (j s) -> p j s", j=31),
                    in1=pc("cf_w_dwT", 31 * c, 31).unsqueeze(2).to_broadcast([128, 31, NS]), op=ALU.mult))
                op(DVE, ["hh0"], ["hh1"], lambda: V.tensor_reduce(out=yv, in_=tmpc.rearrange("p (j s) -> p s j", j=31), axis=AX.X, op=ALU.add))
                op(ACT, ["hh1", "pcol"], ["yT"], lambda c=c: Aeng.activation(out=yT[:, c, 0:N], in_=yv, func=AF.Identity, bias=pc("cf_b_dw", c)))
                a = c % 2
                op(ACT, ["hh1", "pcol"], ["kw%d" % a], lambda c=c, a=a: Aeng.activation(out=ysq[a][:, 0:N], in_=yv, func=AF.Square,
                                                                                     bias=pc("cf_b_dw", c)))
                op(PE, ["yT", "onesE"], [pmeanb], lambda c=c: Pm.matmul(pmean[:, 0:N], lhsT=onesE[:], rhs=yT[:, c, 0:N], start=(c == 0),
                                                                       stop=(c == 15)))
                op(PE, ["kw%d" % a, "onesE"], [pex2b], lambda c=c, a=a: Pm.matmul(pex2[:, 0:N], lhsT=onesE[:], rhs=ysq[a][:, 0:N],
                                                                                start=(c == 0), stop=(c == 15)))
                continue
            pt, pb = bank()
            for gi, (j0, nj, Eg_, En_) in enumerate(GRP):
                op(Eg_, ["ident", "pcol"], ["dg31_%d" % gi], lambda c=c, gi=gi, j0=j0, nj=nj, En_=En_: En_.tensor_tensor(
                    out=dg31[gi][:, 0:nj, :], in0=ident[:].unsqueeze(1).to_broadcast([128, nj, 128]),
                    in1=pc("cf_w_dwT", 31 * c + j0, nj).unsqueeze(2).to_broadcast([128, nj, 128]), op=ALU.mult))

                def mm(c=c, pt=pt, gi=gi, j0=j0, nj=nj):
                    ins = None
                    for jj in range(nj):
                        j = j0 + jj
                        ins = Pm.matmul(pt[:, 0:N], lhsT=dg31[gi][:, jj, :], rhs=A1[:, c, j * stride:j * stride + N], start=(j == 0),
                                        stop=(j == 30))
                    return ins
                op(PE, ["dg31_%d" % gi, "A1"], [pb], mm)
            op(ACT, [pb, "pcol"], ["yT"], lambda c=c, pt=pt: Aeng.activation(out=yT[:, c, 0:N], in_=pt[:, 0:N], func=AF.Identity,
                                                                           bias=pc("cf_b_dw", c)))
            a = c % 2
            op(ACT, [pb, "pcol"], ["kw%d" % a], lambda c=c, pt=pt, a=a: Aeng.activation(out=ysq[a][:, 0:N], in_=pt[:, 0:N], func=AF.Square,
                                                                                   bias=pc("cf_b_dw", c)))
            op(PE, ["yT", "onesE"], [pmeanb], lambda c=c: Pm.matmul(pmean[:, 0:N], lhsT=onesE[:], rhs=yT[:, c, 0:N], start=(c == 0),
                                                                   stop=(c == 15)))
            op(PE, ["kw%d" % a, "onesE"], [pex2b], lambda c=c, a=a: Pm.matmul(pex2[:, 0:N], lhsT=onesE[:], rhs=ysq[a][:, 0:N],
                                                                            start=(c == 0), stop=(c == 15)))
        op(DVE, [pmeanb], ["rn"], lambda: V.tensor_copy(nmr_bc[:, 0:N], pmean[:, 0:N]))
        op(DVE, ["rn"], ["rn"], lambda: V.tensor_tensor(out=rstd_bc[:, 0:N], in0=nmr_bc[:, 0:N], in1=nmr_bc[:, 0:N], op=ALU.mult))
        op(DVE, [pex2b, "rn"], ["rn"], lambda: V.tensor_tensor(out=rstd_bc[:, 0:N], in0=pex2[:, 0:N], in1=rstd_bc[:, 0:N], op=ALU.subtract))
        release(sgot)
        op(ACT, ["rn"], ["rn"], lambda: Aeng.activation(out=rstd_bc[:, 0:N], in_=rstd_bc[:, 0:N], func=AF.Ln, bias=EPS))
        op(ACT, ["rn"], ["rn"], lambda: Aeng.activation(out=rstd_bc[:, 0:N], in_=rstd_bc[:, 0:N], func=AF.Exp, scale=-0.5))
        op(DVE, ["rn"], ["rn"], lambda: V.scalar_tensor_tensor(out=nmr_bc[:, 0:N], in0=nmr_bc[:, 0:N], scalar=-1.0, in1=rstd_bc[:, 0:N],
                                                              op0=ALU.mult, op1=ALU.mult))
        for p in range(4):
            wv, wb = wget(g["cf_w_in"][:, 2 * E + p * 512:2 * E + (p + 1) * 512], 8, 512, key=("cfz", p))

            def ev(fc, pt, pb, p=p):
                ch = 4 * p + fc
                a = ch % 2
                op(ACT, [pb, "pcol"], ["hn%d" % a], lambda: Aeng.activation(out=sig[a][:, 0:N], in_=pt[:, 0:N], func=AF.Silu,
                                                                          bias=pc("cf_b_in", 32 + ch)))
                op(DVE, ["yT", "rn"], ["hh%d" % a], lambda: V.tensor_tensor(out=lt1[a][:, 0:N], in0=yT[:, ch, 0:N], in1=rstd_bc[:, 0:N],
                                                                        op=ALU.mult))
                op(DVE, ["hh%d" % a, "rn"], ["hh%d" % a], lambda: V.tensor_tensor(out=lt1[a][:, 0:N], in0=lt1[a][:, 0:N], in1=nmr_bc[:, 0:N],
                                                                              op=ALU.add))
                op(ACT, ["hh%d" % a, "pcol"], ["qs%d" % a], lambda: Aeng.activation(out=lt2[a][:, 0:N], in_=lt1[a][:, 0:N], func=AF.Silu,
                                                                                  scale=pc("cf_ln_g", ch), bias=pc("cf_ln_b", ch)))
                op(DVE, ["qs%d" % a, "hn%d" % a], ["A2"], lambda: V.tensor_tensor(out=A2[:, ch, 0:N], in0=lt2[a][:, 0:N], in1=sig[a][:, 0:N],
                                                                              op=ALU.mult))
            fm_group(wv, wb, lambda kc: hT[:, kc, 0:N], ["hT"], 8, N, 4, ev)
        if before_outproj is not None:
            before_outproj()
        outproj_resid(1, g["cf_w_out"], ntiles, nt)

    def final_out(ntiles, nt, dst_rows):
        for tt in range(ntiles):
            rs, sn_ = rms_rows(xres[0:nt, tt, :], nt)
            for hf in range(2):
                op(DVE, ["xres", sn_, "fg_bc"], ["hh%d" % hf], lambda hf=hf, tt=tt, rs=rs: V.scalar_tensor_tensor(
                    out=hh[hf][0:nt, :], in0=xres[0:nt, tt, hf * 512:(hf + 1) * 512], scalar=rs,
                    in1=fg_bc[0:nt, hf * 512:(hf + 1) * 512], op0=ALU.mult, op1=ALU.mult))
                dma(SP, ["hh%d" % hf], [], lambda hf=hf, tt=tt: nc.sync.dma_start(
                    out=dst_rows(tt)[:, hf * 512:(hf + 1) * 512], in_=hh[hf][0:nt, :]))

    ssm, ntok, hs, dexp, dec_bc, kwm = (g[k] for k in ["ssm", "ntok", "hs", "dexp", "dec_bc", "kwm"])
    Cs_, nso, mso, mcs, ccs = g["Cs"], g["nso"], g["mso"], g["mcs"], g["ccs"]
    cslot = [C32[:, 4 * i:4 * i + 4, :] for i in range(4)]
    stg = C32[:, 0:4, :].rearrange("p a b -> p (a b)")
    qdiag = A1[:, :, 64:320].rearrange("p c (s m) -> p c s m", s=16)
    cbs = [yT[:, 4 * i:4 * i + 4, :] for i in range(4)]
    szs = qs[0][:].rearrange("p c t -> p (c t)")

    dma(SP, [], ["xres"], lambda: nc.sync.dma_start(out=xres[0:NS, 0, :], in_=xs))
    for j in range(3):
        dma(SP, [], ["stg"], lambda j=j: nc.sync.dma_start(out=stg[j * NS:(j + 1) * NS, :], in_=smc[:, j, :]))
    for half in range(2):
        pt, pb = bank()
        for cc_ in range(8):
            c = half * 8 + cc_
            op(PE, ["stg", "identf"], [pb], lambda c=c, cc_=cc_, pt=pt: Pm.transpose(pt[:, cc_ * 48:(cc_ + 1) * 48], stg[0:48, c * 128:(c + 1) * 128],
                                                                                 identf[0:48, 0:48]))
        op(DVE, [pb], ["A1"], lambda half=half, pt=pt: V.tensor_copy(A1[:, half * 8:(half + 1) * 8, 0:48],
                                                                      pt[:, 0:384].rearrange("p (c r) -> p c r", c=8)))
    layer0_front(1, NS, True, NS)
    stage(2)
    dma(SP, [], [], lambda: nc.sync.dma_start(out=mcs[:, 0:2, :], in_=smc[:, 1:3, :]))
    rows_out(lambda c: A1[:, c, 48:64], ["A1"], NS, lambda q: mcs[:, 2, q * 512:(q + 1) * 512])
    dma(SP, [], ["ssm"], lambda: nc.sync.dma_start(out=ssm[:, 0:4], in_=sm))
    igs, lfs = igt[0:NS, 0, :], lft[0:NS, 0, :]
    c_ = lambda a: ssm[:, 4 * a:4 * a + 4]
    op(DVE, ["ssm", "lft"], ["ssm"], lambda: V.tensor_tensor(out=c_(1), in0=c_(0), in1=lfs, op=ALU.add))
    op(DVE, ["ssm", "igt"], ["ssm"], lambda: V.tensor_tensor(out=c_(2), in0=c_(1), in1=igs, op=ALU.max))
    op(DVE, ["ssm", "igt"], ["ssm"], lambda: V.tensor_tensor(out=c_(3), in0=igs, in1=c_(2), op=ALU.subtract))
    op(ACT, ["ssm"], ["ssm"], lambda: Aeng.activation(out=c_(3), in_=c_(3), func=AF.Exp))
    op(DVE, ["ssm"], ["ssm"], lambda: V.tensor_tensor(out=c_(4), in0=c_(1), in1=c_(2), op=ALU.subtract))
    op(ACT, ["ssm"], ["ssm"], lambda: Aeng.activation(out=c_(4), in_=c_(4), func=AF.Exp))
    op(ACT, ["ssm"], ["ssm"], lambda: Aeng.activation(out=c_(5), in_=c_(2), func=AF.Exp, scale=-1.0))
    dma(SP, ["ssm"], [], lambda: nc.sync.dma_start(out=mso, in_=c_(2)))
    handoff(["stg"], ["cslot0", "cslot1", "cslot2", "cslot3"])
    for h in range(H):
        hsl = slice(h * 512, (h + 1) * 512)
        head_proj(h, 1, NS, NS, NS, True)
        for c in range(4):
            op(DVE, ["qTh", "eye16"], ["qdiag"], lambda c=c, h=h: V.tensor_tensor(
                out=qdiag[:, 4 * h + c, :, :], in0=qTh[:, c, 0:NS].unsqueeze(2).to_broadcast([128, NS, NS]), in1=eye16[:], op=ALU.mult))
        scale_xc_skip(h, NS)
        op(DVE, ["szh"], ["qs0"], lambda h=h: V.tensor_copy(szs[:, h * 64:h * 64 + 64].rearrange("p (c s) -> p c s", c=4), szh[:, :, 0:NS]))
        dma(SP, [], ["ntok"], lambda hsl=hsl: nc.sync.dma_start(out=ntok[:], in_=sn[:, hsl]))
        op(DVE, ["qtok", "ktok"], ["hs"], lambda h=h: V.tensor_tensor(out=hs[:], in0=qtok[:], in1=ktok[0:NS, h, :], op=ALU.mult))
        op(DVE, ["hs"], ["ssm"], lambda h=h: V.tensor_reduce(out=ssm[:, 24 + h:25 + h], in_=hs[:], axis=AX.X, op=ALU.add))
        op(DVE, ["qtok", "ntok"], ["hs"], lambda: V.tensor_tensor(out=hs[:], in0=qtok[:], in1=ntok[:], op=ALU.mult))
        op(DVE, ["hs"], ["ssm"], lambda h=h: V.tensor_reduce(out=ssm[:, 28 + h:29 + h], in_=hs[:], axis=AX.X, op=ALU.add))
        op(DVE, ["ktok", "ssm"], ["ktok"], lambda h=h: V.tensor_scalar(out=ktok[0:NS, h, :], in0=ktok[0:NS, h, :], scalar1=ssm[:, 12 + h:13 + h],
                                                                     scalar2=None, op0=ALU.mult))
        op(DVE, ["ntok", "ssm", "ktok"], ["ntok"], lambda h=h: V.scalar_tensor_tensor(
            out=ntok[:], in0=ntok[:], scalar=ssm[:, 16 + h:17 + h], in1=ktok[0:NS, h, :], op0=ALU.mult, op1=ALU.add))
        dma(SP, ["ntok"], [], lambda hsl=hsl: nc.sync.dma_start(out=nso[:, hsl], in_=ntok[:]))
    op(DVE, ["ssm"], ["ssm"], lambda: V.tensor_tensor(out=c_(8), in0=c_(6), in1=c_(3), op=ALU.mult))
    op(DVE, ["ssm"], ["ssm"], lambda: V.tensor_tensor(out=c_(9), in0=c_(7), in1=c_(4), op=ALU.mult))
    op(DVE, ["ssm"], ["ssm"], lambda: V.tensor_tensor(out=c_(9), in0=c_(9), in1=c_(8), op=ALU.add))
    op(ACT, ["ssm"], ["ssm"], lambda: Aeng.activation(out=c_(9), in_=c_(9), func=AF.Abs))
    op(DVE, ["ssm"], ["ssm"], lambda: V.tensor_tensor(out=c_(9), in0=c_(9), in1=c_(5), op=ALU.max))
    op(DVE, ["ssm"], ["ssm"], lambda: V.reciprocal(out=c_(10), in_=c_(9)))
    op(DVE, ["ssm", "identf"], ["dexp"], lambda: V.tensor_tensor(
        out=dexp[:].rearrange("p (s h) -> p s h", s=NS), in0=identf[0:NS, 0:NS].unsqueeze(2).to_broadcast([NS, NS, 4]),
        in1=c_(4).unsqueeze(1).to_broadcast([NS, NS, 4]), op=ALU.mult))
    pt, pb = bank()
    op(PE, ["dexp", "onesf"], [pb], lambda pt=pt: Pm.matmul(pt[:, 0:64], lhsT=onesf[0:NS, :], rhs=dexp[:], start=True, stop=True))
    op(DVE, [pb], ["dec_bc"], lambda pt=pt: V.tensor_copy(dec_bc[:], pt[:, 0:64]))
    stage(3)
    stg2 = xres[:, 1:3, :].rearrange("p a b -> p (a b)")
    for i in range(4):
        nj = 8 if i < 3 else 6
        rws = nj * NS
        for jj in range(nj):
            dma(SP, [], ["xres"], lambda i=i, jj=jj: nc.sync.dma_start(out=stg2[jj * NS:(jj + 1) * NS, :], in_=scc[:, 8 * i + jj, :]))
        for q4 in range(4):
            pt, pb = bank()
            for cc_ in range(4):
                c = q4 * 4 + cc_
                op(PE, ["xres", "identf"], [pb], lambda c=c, cc_=cc_, pt=pt, rws=rws: Pm.transpose(
                    pt[:, cc_ * 128:cc_ * 128 + rws], stg2[0:rws, c * 128:(c + 1) * 128], identf[0:rws, 0:rws]))
            op(DVE, [pb], ["A2h"], lambda q4=q4, pt=pt, i=i, rws=rws: V.tensor_copy(
                A2[:, q4 * 4:q4 * 4 + 4, 16 + 128 * i:16 + 128 * i + rws], pt[:, :].rearrange("p (c r) -> p c r", c=4)[:, :, 0:rws]))
    handoff(HL, ["cbs%d" % i for i in range(4)])
    qcb, qgot = reserve(4)
    tiles = [(s_, h_) for s_ in range(NS) for h_ in range(H)]

    def c_load(k):
        s_, h_ = tiles[k]
        cs_t = cslot[k % 4]
        dma(SP, [], ["cslot%d" % (k % 4)], lambda: nc.sync.dma_start(out=cs_t, in_=sC[s_, h_].rearrange("(c p) v -> p c v", p=128)))
    c_load(0)
    c_load(1)
    c_load(2)
    for k_i, (s, h) in enumerate(tiles):
        if k_i + 3 < len(tiles):
            c_load(k_i + 3)
        sl = k_i % 4
        cb = k_i % 4
        cs_t = cslot[sl]
        op(ACT, ["cslot%d" % sl], ["cbs%d" % cb], lambda cs_t=cs_t, cb=cb: Aeng.activation(out=cbs[cb], in_=cs_t, func=AF.Copy))
        qp, qpb = qcb[h]

        def mmq(s=s, h=h, cb=cb, qp=qp):
            ins = None
            for c in range(4):
                ins = Pm.matmul(qp[0:NS, :], lhsT=qdiag[:, 4 * h + c, s, :], rhs=cbs[cb][:, c, :], start=(s == 0 and c == 0),
                                stop=(s == NS - 1 and c == 3))
            return ins
        op(PE, ["qdiag", "cbs%d" % cb], [qpb], mmq)
        km = kwm[k_i % 2]
        kn = "kwm%d" % (k_i % 2)
        op(DVE, ["ktok", "identf"], [kn], lambda s=s, h=h, km=km: V.tensor_scalar(
            out=km[:], in0=ktok[0:NS, h, :], scalar1=identf[0:NS, s:s + 1], scalar2=None, op0=ALU.mult))
        for c in range(4):
            ptC, pbC = bank()
            op(PE, [kn, "vtok"], [pbC], lambda c=c, ptC=ptC, km=km, h=h: Pm.matmul(
                ptC[:, :], lhsT=km[:, c * 128:(c + 1) * 128], rhs=vtok[0:NS, h, :], start=True, stop=True))
            op(DVE, [pbC, "dec_bc", "cslot%d" % sl], ["cslot%d" % sl], lambda c=c, ptC=ptC, cs_t=cs_t, s=s, h=h: V.scalar_tensor_tensor(
                out=cs_t[:, c, :], in0=cs_t[:, c, :], scalar=dec_bc[:, s * 4 + h:s * 4 + h + 1], in1=ptC[:, :], op0=ALU.mult, op1=ALU.add))
        dma(POOL, ["cslot%d" % sl], [], lambda s=s, h=h, cs_t=cs_t: G.dma_start(
            out=Cs_[s, h].rearrange("(c p) v -> p c v", p=128), in_=cs_t))
    stage(4)
    for h in range(H):
        qp, qpb = qcb[h]
        op(DVE, [qpb, "ssm"], ["hs"], lambda qp=qp, h=h: V.tensor_scalar(out=hs[:], in0=qp[0:NS, :], scalar1=ssm[:, 16 + h:17 + h],
                                                                     scalar2=None, op0=ALU.mult))
        op(DVE, ["vtok", "ssm", "hs"], ["hs"], lambda h=h: V.scalar_tensor_tensor(
            out=hs[:], in0=vtok[0:NS, h, :], scalar=ssm[:, 32 + h:33 + h], in1=hs[:], op0=ALU.mult, op1=ALU.add))
        op(DVE, ["hs", "ssm"], ["hs"], lambda h=h: V.tensor_scalar(out=hs[:], in0=hs[:], scalar1=ssm[:, 40 + h:41 + h], scalar2=None,
                                                               op0=ALU.mult))
        ln_rows(0, NS, hs[:], "hs", hn[0][0:NS, :])
        pt, pb = bank()
        ptb = pt[:].bitcast(BF16).rearrange("p (c t) -> p c t", c=8)
        for c in range(4):
            op(PE, ["hn0", "ident"], [pb], lambda c=c, ptb=ptb: Pm.transpose(ptb[:, c, 0:NS], hn[0][0:NS, c * 128:(c + 1) * 128],
                                                                          ident[0:NS, 0:NS]))
        for c in range(4):
            ch = 4 * h + c
            op(DVE, [pb, "pcol", "A2"], ["A2"], lambda c=c, ch=ch, ptb=ptb: V.scalar_tensor_tensor(
                out=A2[:, ch, 0:NS], in0=ptb[:, c, 0:NS], scalar=pc("ml_ln_g", ch), in1=A2[:, ch, 0:NS], op0=ALU.mult, op1=ALU.add))
            op(DVE, ["A2", "qs0"], ["A2"], lambda c=c, ch=ch, h=h: V.tensor_tensor(
                out=A2[:, ch, 0:NS], in0=A2[:, ch, 0:NS], in1=szs[:, h * 64 + c * 16:h * 64 + c * 16 + 16], op=ALU.mult))
    release(qgot)
    outproj_resid(0, g["ml_w_out"], 1, NS)
    stage(5)
    handoff(["cslot0", "cslot1", "cslot2", "cslot3"], ["stg"])
    handoff(["qdiag"], ["A1"])
    handoff(["cbs%d" % i for i in range(4)], HL + ["yT"])
    layer1(1, NS, True, NS)
    dma(SP, [], [], lambda: nc.sync.dma_start(out=ccs[:, 0:29, :], in_=scc[:, 1:30, :]))
    rows_out(lambda c: A2[:, c, 496:512], ["A2h"], NS, lambda q: ccs[:, 29, q * 512:(q + 1) * 512])
    final_out(1, NS, lambda tt: g["ys"])

    stage(6)
    for l in range(2):
        for hf in range(2):
            pt, pb = bank()
            op(PE, ["gate_t%d" % l, "sel16"], [pb], lambda pt=pt, hf=hf, l=l: Pm.matmul(
                pt[:, :], lhsT=sel16[:], rhs=gate_t[l][0:R, hf * 512:(hf + 1) * 512], start=True, stop=True))
            op(DVE, [pb], ["gate_t%d" % l], lambda pt=pt, hf=hf, l=l: V.tensor_copy(gate_t[l][:, hf * 512:(hf + 1) * 512], pt[:, :]))
    handoff(["stg", "cslot0", "cslot1", "cslot2", "cslot3"], ["C32_%d_%d" % (h, c) for h in range(H) for c in range(4)])
    handoff(["ntok", "hs", "qtok", "kwm0", "kwm1"], ["WtAll", "wintAll"])
    handoff(["A2h"], ["A2"])
    for h in range(H):
        op(POOL, [], ["C32_%d_%d" % (h, c) for c in range(4)], lambda h=h: G.memset(C32[:, 4 * h:4 * h + 4, :], 0.0))
    op(DVE, [], ["n32"], lambda: V.memset(n32[:], 0.0))
    op(DVE, [], ["mprev"], lambda: V.memset(mprev[:], 0.0))
    op(DVE, [], ["mhist"], lambda: V.memset(mhist[:], 0.0))
    op(DVE, [], ["chist"], lambda: V.memset(chist[:], 0.0))
    def early_norm(b):
        tb0 = b * TB
        norm_mod(0, 4, 128, False, stage_src=lambda tt: xp[tb0 + tt * 128:tb0 + (tt + 1) * 128, :])

    early_norm(0)
    for blk in range(NBLK):
        t0 = blk * TB
        for tt in range(4):
            dma(SP, [], ["xres"], lambda tt=tt: nc.sync.dma_start(out=xres[:, tt, :], in_=xp[t0 + tt * 128:t0 + (tt + 1) * 128, :]))
        op(DVE, ["mhist"], ["A1"], lambda: V.tensor_copy(A1[:, :, 0:3], mhist[:]))
        layer0_front(4, 128, False, 1, skip_norm=True)
        op(DVE, ["A1"], ["mhist"], lambda: V.tensor_copy(mhist[:], A1[:, :, TB:TB + 3]))
        if blk == NBLK - 1:
            rows_out(lambda c: A1[:, c, TB:TB + 3], ["A1"], 3, lambda q: g["mcp"][:, q * 512:(q + 1) * 512])
        if blk == 0:
            stage(7)
        if blk == 0:
            stage(71)
        for h in range(H):
            head_proj(h, 4, 128, TB, 1, False, between=(mchain if h == 0 else None))
            scale_xc_skip(h, TB)
            if blk == 0 and h == 0:
                stage(72)
            prompt_cell(h, 4)
            if blk == 0 and h == 0:
                stage(73)
        if blk == 0:
            stage(8)
        outproj_resid(0, g["ml_w_out"], 4, 128)
        if blk == 0:
            stage(9)
        op(DVE, ["chist"], ["A1"], lambda: V.tensor_copy(A1[:, :, 0:30], chist[:]))
        layer1(4, 128, False, 1, before_outproj=((lambda blk=blk: early_norm(blk + 1)) if blk + 1 < NBLK else None))
        op(DVE, ["A1"], ["chist"], lambda: V.tensor_copy(chist[:], A1[:, :, TB:TB + 30]))
        if blk == NBLK - 1:
            rows_out(lambda c: A1[:, c, TB:TB + 30], ["A1"], 30, lambda q: g["ccp"][:, q * 512:(q + 1) * 512])
        final_out(4, 128, lambda tt: g["yp"][t0 + tt * 128:t0 + (tt + 1) * 128, :])
        if blk == 0:
            stage(10)
    for h in range(H):
        dma(SP, ["C32_%d_%d" % (h, c) for c in range(4)], [], lambda h=h: nc.sync.dma_start(out=g["Cp"][h].rearrange("(c p) v -> p c v", p=128),
                                                                 in_=C32[:, 4 * h:4 * h + 4, :]))
    dma(SP, ["n32"], [], lambda: nc.sync.dma_start(out=g["npo"].rearrange("h (c p) -> p h c", p=128),
                                                   in_=n32[:].rearrange("p (h c) -> p h c", h=4), allow_slow_non_contiguous=True))
    dma(SP, ["mprev"], [], lambda: nc.sync.dma_start(out=g["mpo"], in_=mprev[0:1, :]))
    T.finish()
    if not dry:
        print("emit: ops=%d waits=%d weight_panels=%d" % (T.nops, T.nwaits, len(ws["descs"])))


def _cols(v):
    v = np.ascontiguousarray(v, dtype=np.float32).reshape(-1)
    return v.reshape(-1, 128).T


_NC_CACHE = {}
_PREP_ONLY = False


def kernel(x_prompt, x_sample, c_prompt, c_sample, state_mlstm_C, state_mlstm_n, state_mlstm_m, state_mlstm_conv,
           state_conf_conv, norm_g, w_ada, b_ada, ml_w_in, ml_w_conv, ml_b_conv, ml_w_q, ml_w_k, ml_w_v, ml_w_ig, ml_b_ig,
           ml_w_fg, ml_b_fg, ml_ln_g, ml_skip, ml_w_out, cf_w_in, cf_b_in, cf_w_dw, cf_b_dw, cf_ln_g, cf_ln_b, cf_w_out, final_g):
    f = lambda a: np.ascontiguousarray(np.asarray(a), dtype=np.float32)
    pcol = np.zeros((128, NPCOL), np.float32)

    def put(name, arr):
        pcol[:, PCOL[name]:PCOL[name] + arr.shape[1]] = arr
    put("norm_g0", _cols(norm_g[0])); put("norm_g1", _cols(norm_g[1]))
    put("b_ada0", _cols(b_ada[0])); put("b_ada1", _cols(b_ada[1]))
    put("ml_b_conv", _cols(ml_b_conv[0])); put("ml_ln_g", _cols(ml_ln_g[0])); put("ml_skip", _cols(ml_skip[0]))
    put("cf_b_in", _cols(cf_b_in[0])); put("cf_b_dw", _cols(cf_b_dw[0])); put("cf_ln_g", _cols(cf_ln_g[0])); put("cf_ln_b", _cols(cf_ln_b[0]))
    wc = f(ml_w_conv[0]).T.reshape(16, 128, 4).transpose(1, 0, 2).reshape(128, 64)
    put("ml_w_convT", wc)
    wd = f(cf_w_dw[0]).T.reshape(16, 128, 31).transpose(1, 0, 2).reshape(128, 496)
    put("cf_w_dwT", wd)
    shared = dict(
        pcold=pcol, w_ada=f(w_ada), b_gate=f(np.asarray(b_ada)[:, 2 * D:3 * D]), ml_w_in=f(ml_w_in[0]),
        ml_w_q=f(ml_w_q[0]), ml_w_k=f(ml_w_k[0]), ml_w_v=f(ml_w_v[0]),
        w_gates=f(np.concatenate([np.asarray(ml_w_ig[0]), np.asarray(ml_w_fg[0])], axis=1)),
        b_gates=f(np.concatenate([np.asarray(ml_b_ig[0]), np.asarray(ml_b_fg[0])], axis=0)),
        ml_w_out=f(ml_w_out[0]), cf_w_in=f(cf_w_in[0]), cf_w_out=f(cf_w_out[0]), final_g=f(final_g))
    xpn, xsn, cpn, csn = f(x_prompt), f(x_sample), f(c_prompt), f(c_sample)
    sCn, snn, smn, smcn, sccn = f(state_mlstm_C), f(state_mlstm_n), f(state_mlstm_m), f(state_mlstm_conv), f(state_conf_conv)
    in_maps = []
    for i in range(NCORES):
        s0, s1 = i * NS, (i + 1) * NS
        m = dict(shared)
        m.update(xp=xpn[i], xs=xsn[s0:s1, 0, :], cc=np.concatenate([csn[s0:s1], cpn[i:i + 1]], axis=0),
                 sC=sCn[0, s0:s1], sn=snn[0, s0:s1].reshape(NS, E), sm=smn[0, s0:s1], smc=smcn[0, s0:s1], scc=sccn[0, s0:s1])
        in_maps.append(m)
    if _PREP_ONLY:
        return in_maps
    if "nc" not in _NC_CACHE:
        _NC_CACHE["nc"] = build_nc()
    res = run_bass_kernel_spmd(_NC_CACHE["nc"], in_maps, core_ids=list(range(NCORES)))
    r = res.results
    st = lambda k: np.stack([np.asarray(r[i][k], dtype=np.float32) for i in range(NCORES)], axis=0)
    cat = lambda k: np.concatenate([np.asarray(r[i][k], dtype=np.float32) for i in range(NCORES)], axis=0)
    y_prompt = st("yp")
    y_sample = cat("ys").reshape(NCORES * NS, 1, D)
    C_p = st("Cp")[None]
    C_s = cat("Cs")[None]
    n_p = st("npo")[None]
    n_s = cat("nso").reshape(NCORES * NS, H, DH)[None]
    m_p = st("mpo").reshape(NCORES, H)[None]
    m_s = cat("mso")[None]
    mc_p = st("mcp")[None]
    mc_s = cat("mcs")[None]
    cc_p = st("ccp")[None]
    cc_s = cat("ccs")[None]
    return (y_prompt, y_sample, C_p, C_s, n_p, n_s, m_p, m_s, mc_p, mc_s, cc_p, cc_s)
```
